# Optimizing a Trainium2 kernel written in Bass

```python
import math
import jax
import jax.numpy as jnp
from jax import lax
import numpy as np

D_MODEL = 1024
BATCH = 2
SEQ = 8192
DEPTH = 4

N_MIXERS = 4
N_PER_MIXER = tuple((DEPTH - m + N_MIXERS - 1) // N_MIXERS for m in range(N_MIXERS))
PLE_DIM = 256
ROPE_THETA = 10000.0
EPS = 1e-6
Q_BLOCK = 128
D_FF = (-((-8 * D_MODEL) // 3) + 255) // 256 * 256

GLA_HEADS = 4
GLA_DK = D_MODEL // 2
GLA_DV = D_MODEL
GLA_HK = GLA_DK // GLA_HEADS
GLA_HV = GLA_DV // GLA_HEADS
GLA_RANK = 16
GLA_TAU = 16.0
GLA_CHUNK = 64
GLA_IN = 2 * GLA_DK + 2 * GLA_DV + 2 * GLA_RANK

DIFF_HEAD_DIM = 64
DIFF_HEADS = D_MODEL // (2 * DIFF_HEAD_DIM)
DIFF_IN = 3 * D_MODEL

SSD_D_INNER = 2 * D_MODEL
SSD_HEAD_DIM = 64
SSD_HEADS = SSD_D_INNER // SSD_HEAD_DIM
SSD_GROUPS = 8
SSD_STATE = 128
SSD_CONV = 5
SSD_CHUNK = 128
SSD_CONV_DIM = SSD_D_INNER + 2 * SSD_GROUPS * SSD_STATE
SSD_IN = SSD_D_INNER + SSD_CONV_DIM + 2 * SSD_HEADS

DIL_PAIRS = ((128, 1), (512, 4), (2048, 16))
DIL_GROUPS = len(DIL_PAIRS)
DIL_HEADS = 16
DIL_HEAD_DIM = 64
DIL_WIDTH = DIL_HEADS * DIL_HEAD_DIM
DIL_IN = 3 * DIL_GROUPS * DIL_WIDTH

kernel_name = 'hybrid_bidir_interleaved_encoder'

F32 = jnp.float32


def split_cols(x, sizes):
    out, start = [], 0
    for s in sizes:
        out.append(x[..., start:start + s])
        start += s
    return out


def rms_norm(x, g):
    xf = x.astype(F32)
    y = xf * lax.rsqrt(jnp.mean(xf * xf, axis=-1, keepdims=True) + EPS)
    return (y * g.astype(F32)).astype(x.dtype)


def rope(x, pos):
    hd = x.shape[-1]
    half = hd // 2
    inv = ROPE_THETA ** (-jnp.arange(half, dtype=F32) * 2.0 / hd)
    ang = pos.astype(F32)[:, None] * inv[None, :]
    cos = jnp.cos(ang)[:, None, :]
    sin = jnp.sin(ang)[:, None, :]
    xf = x.astype(F32)
    x1, x2 = xf[..., :half], xf[..., half:]
    return jnp.concatenate([x1 * cos - x2 * sin, x2 * cos + x1 * sin], axis=-1).astype(x.dtype)


def swiglu(h, w_in, w_out):
    g, u = split_cols(h @ w_in, [D_FF, D_FF])
    return (jax.nn.silu(g) * u) @ w_out


def gla_chunked(q, k, v, log_a, strict):
    bsz, nh, seq, dk = q.shape
    dv = v.shape[-1]
    c = GLA_CHUNK
    n = seq // c
    qf = q.astype(F32).reshape(bsz, nh, n, c, dk)
    kf = k.astype(F32).reshape(bsz, nh, n, c, dk)
    vf = v.astype(F32).reshape(bsz, nh, n, c, dv)
    b = jnp.cumsum(log_a.astype(F32).reshape(bsz, nh, n, c, dk), axis=3)
    b_last = b[:, :, :, -1:, :]
    q_in = qf * jnp.exp(b)
    scores = jnp.einsum('bhncd,bhnjd->bhncj', q_in, kf * jnp.exp(-b))
    mask = jnp.tril(jnp.ones((c, c), dtype=bool), -1 if strict else 0)
    scores = jnp.where(mask, scores, 0.0)
    o_intra = jnp.einsum('bhncj,bhnje->bhnce', scores, vf)
    u = jnp.einsum('bhncd,bhnce->bhnde', kf * jnp.exp(b_last - b), vf)
    decay = jnp.exp(b_last[:, :, :, 0, :])

    def step(s_prev, inp):
        u_n, a_n = inp
        return s_prev * a_n[..., None] + u_n, s_prev

    s0 = jnp.zeros((bsz, nh, dk, dv), F32)
    _, s_prev = lax.scan(step, s0, (jnp.moveaxis(u, 2, 0), jnp.moveaxis(decay, 2, 0)))
    s_prev = jnp.moveaxis(s_prev, 0, 2)
    o_inter = jnp.einsum('bhncd,bhnde->bhnce', q_in, s_prev)
    return (o_intra + o_inter).reshape(bsz, nh, seq, dv)


def gla_mixer(h, w_in, w_gate_f, b_gate_f, w_gate_b, b_gate_b, g_out, w_out):
    bsz, seq, _ = h.shape
    q, k, v, r, zf, zb = split_cols(h @ w_in, [GLA_DK, GLA_DK, GLA_DV, GLA_DV, GLA_RANK, GLA_RANK])

    def heads(t, hd):
        return t.reshape(bsz, seq, GLA_HEADS, hd).transpose(0, 2, 1, 3)

    qh = heads(q, GLA_HK) * (GLA_HK ** -0.5)
    kh = heads(k, GLA_HK)
    vh = heads(v, GLA_HV)
    la_f = heads(jax.nn.log_sigmoid((zf @ w_gate_f + b_gate_f).astype(F32)) / GLA_TAU, GLA_HK)
    la_b = heads(jax.nn.log_sigmoid((zb @ w_gate_b + b_gate_b).astype(F32)) / GLA_TAU, GLA_HK)

    def flip(t):
        return jnp.flip(t, axis=2)

    o = gla_chunked(qh, kh, vh, la_f, False) + flip(
        gla_chunked(flip(qh), flip(kh), flip(vh), flip(la_b), True))
    o = rms_norm(o.transpose(0, 2, 1, 3).astype(h.dtype), g_out).reshape(bsz, seq, GLA_DV)
    return (o * jax.nn.silu(r)) @ w_out


def diff_attn_mixer(h, pos, w_in, g_q, g_k, lam_q1, lam_k1, lam_q2, lam_k2, g_sub, w_out, lambda_init):
    bsz, seq, _ = h.shape
    nh, dh = DIFF_HEADS, DIFF_HEAD_DIM
    q, k, v = split_cols(h @ w_in, [D_MODEL, D_MODEL, D_MODEL])
    q = rms_norm(q.reshape(bsz, seq, nh, 2, dh), g_q)
    k = rms_norm(k.reshape(bsz, seq, nh, 2, dh), g_k)
    q = rope(q.reshape(bsz, seq, 2 * nh, dh), pos).reshape(bsz, seq, nh, 2, dh)
    k = rope(k.reshape(bsz, seq, 2 * nh, dh), pos).reshape(bsz, seq, nh, 2, dh)
    lam = (jnp.exp(jnp.sum(lam_q1.astype(F32) * lam_k1.astype(F32)))
           - jnp.exp(jnp.sum(lam_q2.astype(F32) * lam_k2.astype(F32))) + lambda_init)
    nb = seq // Q_BLOCK
    qb = q.reshape(bsz, nb, Q_BLOCK, nh, 2, dh).transpose(1, 0, 3, 4, 2, 5)
    kt = k.transpose(0, 2, 3, 1, 4)
    vt = v.reshape(bsz, seq, nh, 2 * dh).transpose(0, 2, 1, 3)
    scale = dh ** -0.5

    def block(q_blk):
        s = jnp.einsum('bhtqd,bhtkd->bhtqk', q_blk, kt).astype(F32) * scale
        pr = jax.nn.softmax(s, axis=-1)
        a = pr[:, :, 0] - lam * pr[:, :, 1]
        return jnp.einsum('bhqk,bhkd->bhqd', a.astype(vt.dtype), vt)

    o = lax.map(block, qb)
    o = o.transpose(1, 0, 3, 2, 4).reshape(bsz, seq, nh, 2 * dh)
    o = rms_norm(o, g_sub) * (1.0 - lambda_init)
    return o.reshape(bsz, seq, D_MODEL) @ w_out


def depthwise_conv_centered(x, w, b):
    width, ch = w.shape
    y = lax.conv_general_dilated(
        x, w[:, None, :].astype(x.dtype), window_strides=(1,),
        padding=[((width - 1) // 2, width // 2)],
        dimension_numbers=('NWC', 'WIO', 'NWC'), feature_group_count=ch)
    return y + b


def ssd_chunked(x, dt, a, bm, cm, strict):
    bsz, seq, nh, hp = x.shape
    ng, ns = bm.shape[2], bm.shape[3]
    hg = nh // ng
    c = SSD_CHUNK
    nc = seq // c
    xdt = (x.astype(F32) * dt[..., None]).reshape(bsz, nc, c, ng, hg, hp)
    acum = jnp.cumsum((dt * a).reshape(bsz, nc, c, ng, hg), axis=2)
    bc = bm.astype(F32).reshape(bsz, nc, c, ng, ns)
    cc = cm.astype(F32).reshape(bsz, nc, c, ng, ns)
    mask = jnp.tril(jnp.ones((c, c), dtype=bool), -1 if strict else 0)[:, :, None, None]
    diff = acum[:, :, :, None] - acum[:, :, None, :]
    lmat = jnp.where(mask, jnp.exp(jnp.where(mask, diff, 0.0)), 0.0)
    cb = jnp.einsum('bnigs,bnjgs->bnijg', cc, bc)
    y_diag = jnp.einsum('bnijg,bnijgh,bnjghp->bnighp', cb, lmat, xdt)
    a_last = acum[:, :, -1]
    decay_states = jnp.exp(a_last[:, :, None] - acum)
    states = jnp.einsum('bnjgs,bnjgh,bnjghp->bnghps', bc, decay_states, xdt)

    def step(s_prev, inp):
        st, dec = inp
        return s_prev * dec[..., None, None] + st, s_prev

    s0 = jnp.zeros((bsz, ng, hg, hp, ns), F32)
    _, prev = lax.scan(step, s0, (jnp.moveaxis(states, 1, 0), jnp.moveaxis(jnp.exp(a_last), 1, 0)))
    prev = jnp.moveaxis(prev, 0, 1)
    y_off = jnp.einsum('bnigs,bnigh,bnghps->bnighp', cc, jnp.exp(acum), prev)
    return (y_diag + y_off).reshape(bsz, seq, nh, hp)


def ssd_mixer(h, w_in, conv_w, conv_b, dt_bias_f, a_log_f, dt_bias_b, a_log_b, d_skip, g_norm, w_out):
    bsz, seq, _ = h.shape
    z, xbc, dt_f, dt_b = split_cols(h @ w_in, [SSD_D_INNER, SSD_CONV_DIM, SSD_HEADS, SSD_HEADS])
    xbc = jax.nn.silu(depthwise_conv_centered(xbc, conv_w, conv_b))
    xs, bm, cm = split_cols(xbc, [SSD_D_INNER, SSD_GROUPS * SSD_STATE, SSD_GROUPS * SSD_STATE])
    xs = xs.reshape(bsz, seq, SSD_HEADS, SSD_HEAD_DIM)
    bm = bm.reshape(bsz, seq, SSD_GROUPS, SSD_STATE)
    cm = cm.reshape(bsz, seq, SSD_GROUPS, SSD_STATE)
    dtf = jax.nn.softplus((dt_f + dt_bias_f).astype(F32))
    dtb = jax.nn.softplus((dt_b + dt_bias_b).astype(F32))
    af = -jnp.exp(a_log_f.astype(F32))
    ab = -jnp.exp(a_log_b.astype(F32))

    def flip(t):
        return jnp.flip(t, axis=1)

    y = ssd_chunked(xs, dtf, af, bm, cm, False) + flip(
        ssd_chunked(flip(xs), flip(dtb), ab, flip(bm), flip(cm), True))
    y = y + xs.astype(F32) * d_skip.astype(F32)[:, None]
    y = y.astype(h.dtype).reshape(bsz, seq, SSD_D_INNER) * jax.nn.silu(z)
    y = rms_norm(y.reshape(bsz, seq, SSD_GROUPS, SSD_D_INNER // SSD_GROUPS),
                 g_norm.reshape(SSD_GROUPS, SSD_D_INNER // SSD_GROUPS))
    return y.reshape(bsz, seq, SSD_D_INNER) @ w_out


def dilated_group(q, k, v, window, dilation):
    bsz, seq, nh, hd = q.shape
    n_side = (window // 2) // dilation
    offsets = dilation * jnp.arange(-n_side, n_side + 1, dtype=jnp.int32)
    nb = seq // Q_BLOCK
    qb = q.reshape(bsz, nb, Q_BLOCK, nh, hd).swapaxes(0, 1)
    starts = jnp.arange(nb, dtype=jnp.int32) * Q_BLOCK
    scale = hd ** -0.5

    def block(args):
        q_blk, start = args
        idx = (start + jnp.arange(Q_BLOCK, dtype=jnp.int32))[:, None] + offsets[None, :]
        valid = (idx >= 0) & (idx < seq)
        idx = jnp.clip(idx, 0, seq - 1)
        kg = k[:, idx]
        vg = v[:, idx]
        s = jnp.einsum('bqhd,bqmhd->bqhm', q_blk, kg).astype(F32) * scale
        s = jnp.where(valid[None, :, None, :], s, -jnp.inf)
        m = jnp.max(s, axis=-1, keepdims=True)
        e = jnp.exp(s - m)
        l = jnp.sum(e, axis=-1, keepdims=True)
        o = jnp.einsum('bqhm,bqmhd->bqhd', (e / l).astype(vg.dtype), vg)
        return o, (m + jnp.log(l))[..., 0]

    o, lse = lax.map(block, (qb, starts))
    return (o.swapaxes(0, 1).reshape(bsz, seq, nh, hd),
            lse.swapaxes(0, 1).reshape(bsz, seq, nh))


def dilated_mixer(h, pos, w_in, g_q, g_k, w_out):
    bsz, seq, _ = h.shape
    qkv = (h @ w_in).reshape(bsz, seq, 3, DIL_GROUPS, DIL_HEADS, DIL_HEAD_DIM)
    q = rms_norm(qkv[:, :, 0], g_q[:, None, :])
    k = rms_norm(qkv[:, :, 1], g_k[:, None, :])
    v = qkv[:, :, 2]
    flat = (bsz, seq, DIL_GROUPS * DIL_HEADS, DIL_HEAD_DIM)
    q = rope(q.reshape(flat), pos).reshape(bsz, seq, DIL_GROUPS, DIL_HEADS, DIL_HEAD_DIM)
    k = rope(k.reshape(flat), pos).reshape(bsz, seq, DIL_GROUPS, DIL_HEADS, DIL_HEAD_DIM)
    outs, lses = [], []
    for gi, (window, dilation) in enumerate(DIL_PAIRS):
        o, lse = dilated_group(q[:, :, gi], k[:, :, gi], v[:, :, gi], window, dilation)
        outs.append(o.astype(F32))
        lses.append(lse)
    wts = jax.nn.softmax(jnp.stack(lses, axis=0), axis=0)
    o = jnp.sum(wts[..., None] * jnp.stack(outs, axis=0), axis=0)
    return o.astype(h.dtype).reshape(bsz, seq, DIL_WIDTH) @ w_out


def setup_inputs(seed: int = 0) -> dict:
    key = jax.random.key(seed)
    ks = iter(jax.random.split(key, 48))
    n_a, n_b, n_c, n_d = N_PER_MIXER

    def nrm(shape, std):
        return jax.random.normal(next(ks), shape, F32) * std

    def dense(shape):
        return nrm(shape, shape[-2] ** -0.5)

    def gain(shape):
        return 1.0 + nrm(shape, 0.05)

    x = nrm((BATCH, SEQ, D_MODEL), 1.0)
    p = nrm((DEPTH, BATCH, SEQ, PLE_DIM), 1.0)

    def dt_bias(shape):
        dt = jnp.exp(jax.random.uniform(next(ks), shape, F32, math.log(1e-3), math.log(1e-1)))
        return dt + jnp.log(-jnp.expm1(-dt))

    def a_log(shape):
        return jnp.log(jax.random.uniform(next(ks), shape, F32, 1.0, 16.0))

    return {
        'x': x,
        'p': p,
        'norm_mix': gain((DEPTH, D_MODEL)),
        'norm_ffn': gain((DEPTH, D_MODEL)),
        'ffn_w_in': dense((DEPTH, D_MODEL, 2 * D_FF)),
        'ffn_w_out': dense((DEPTH, D_FF, D_MODEL)),
        'ple_norm': gain((DEPTH, D_MODEL)),
        'ple_w_gate': dense((DEPTH, D_MODEL, D_MODEL)),
        'ple_w_proj': dense((DEPTH, PLE_DIM, D_MODEL)),
        'gla_w_in': dense((n_a, D_MODEL, GLA_IN)),
        'gla_w_gate_f': dense((n_a, GLA_RANK, GLA_DK)),
        'gla_b_gate_f': nrm((n_a, GLA_DK), 0.01),
        'gla_w_gate_b': dense((n_a, GLA_RANK, GLA_DK)),
        'gla_b_gate_b': nrm((n_a, GLA_DK), 0.01),
        'gla_g_out': gain((n_a, GLA_HV)),
        'gla_w_out': dense((n_a, GLA_DV, D_MODEL)),
        'diff_w_in': dense((n_b, D_MODEL, DIFF_IN)),
        'diff_g_q': gain((n_b, 2, DIFF_HEAD_DIM)),
        'diff_g_k': gain((n_b, 2, DIFF_HEAD_DIM)),
        'diff_lam_q1': nrm((n_b, DIFF_HEAD_DIM), 0.1),
        'diff_lam_k1': nrm((n_b, DIFF_HEAD_DIM), 0.1),
        'diff_lam_q2': nrm((n_b, DIFF_HEAD_DIM), 0.1),
        'diff_lam_k2': nrm((n_b, DIFF_HEAD_DIM), 0.1),
        'diff_g_sub': gain((n_b, 2 * DIFF_HEAD_DIM)),
        'diff_w_out': dense((n_b, D_MODEL, D_MODEL)),
        'ssd_w_in': dense((n_c, D_MODEL, SSD_IN)),
        'ssd_conv_w': nrm((n_c, SSD_CONV, SSD_CONV_DIM), SSD_CONV ** -0.5),
        'ssd_conv_b': nrm((n_c, SSD_CONV_DIM), 0.01),
        'ssd_dt_bias_f': dt_bias((n_c, SSD_HEADS)),
        'ssd_a_log_f': a_log((n_c, SSD_HEADS)),
        'ssd_dt_bias_b': dt_bias((n_c, SSD_HEADS)),
        'ssd_a_log_b': a_log((n_c, SSD_HEADS)),
        'ssd_d': gain((n_c, SSD_HEADS)),
        'ssd_g_norm': gain((n_c, SSD_D_INNER)),
        'ssd_w_out': dense((n_c, SSD_D_INNER, D_MODEL)),
        'dil_w_in': dense((n_d, D_MODEL, DIL_IN)),
        'dil_g_q': gain((n_d, DIL_GROUPS, DIL_HEAD_DIM)),
        'dil_g_k': gain((n_d, DIL_GROUPS, DIL_HEAD_DIM)),
        'dil_w_out': dense((n_d, DIL_WIDTH, D_MODEL)),
    }


def reference(x, p, norm_mix, norm_ffn, ffn_w_in, ffn_w_out, ple_norm, ple_w_gate, ple_w_proj,
              gla_w_in, gla_w_gate_f, gla_b_gate_f, gla_w_gate_b, gla_b_gate_b, gla_g_out, gla_w_out,
              diff_w_in, diff_g_q, diff_g_k, diff_lam_q1, diff_lam_k1, diff_lam_q2, diff_lam_k2,
              diff_g_sub, diff_w_out,
              ssd_w_in, ssd_conv_w, ssd_conv_b, ssd_dt_bias_f, ssd_a_log_f, ssd_dt_bias_b,
              ssd_a_log_b, ssd_d, ssd_g_norm, ssd_w_out,
              dil_w_in, dil_g_q, dil_g_k, dil_w_out):
    seq = x.shape[1]
    pos = jnp.arange(seq, dtype=jnp.int32)
    h = x
    for i in range(DEPTH):
        kind = i % N_MIXERS
        j = i // N_MIXERS
        hn = rms_norm(h, norm_mix[i])
        if kind == 0:
            mix = gla_mixer(hn, gla_w_in[j], gla_w_gate_f[j], gla_b_gate_f[j], gla_w_gate_b[j],
                            gla_b_gate_b[j], gla_g_out[j], gla_w_out[j])
        elif kind == 1:
            lambda_init = 0.8 - 0.6 * math.exp(-0.3 * i)
            mix = diff_attn_mixer(hn, pos, diff_w_in[j], diff_g_q[j], diff_g_k[j], diff_lam_q1[j],
                                  diff_lam_k1[j], diff_lam_q2[j], diff_lam_k2[j], diff_g_sub[j],
                                  diff_w_out[j], lambda_init)
        elif kind == 2:
            mix = ssd_mixer(hn, ssd_w_in[j], ssd_conv_w[j], ssd_conv_b[j], ssd_dt_bias_f[j],
                            ssd_a_log_f[j], ssd_dt_bias_b[j], ssd_a_log_b[j], ssd_d[j],
                            ssd_g_norm[j], ssd_w_out[j])
        else:
            mix = dilated_mixer(hn, pos, dil_w_in[j], dil_g_q[j], dil_g_k[j], dil_w_out[j])
        h = h + mix
        h = h + swiglu(rms_norm(h, norm_ffn[i]), ffn_w_in[i], ffn_w_out[i])
        gate = jax.nn.sigmoid(rms_norm(h, ple_norm[i]) @ ple_w_gate[i])
        h = h + gate * (p[i] @ ple_w_proj[i])
    return h
```

```python
import math
import os
import numpy as np
import ml_dtypes
import concourse.bass as bass
import concourse.mybir as mybir
from concourse.bass_utils import run_bass_kernel_spmd
from contextlib import ExitStack

F32 = mybir.dt.float32
BF16 = mybir.dt.bfloat16
AF = mybir.ActivationFunctionType
ALU = mybir.AluOpType
AX = mybir.AxisListType
NPBF = ml_dtypes.bfloat16

NDS = 8
EPS = 1e-6
TOK = 2048
NT = 16
D = 1024
DFF = 2816
SEQ = 8192


class Prog:
    def __init__(self, nc, es):
        self.nc, self.es = nc, es
        self.q = {e: [] for e in ('pe', 'act', 'dve', 'pool', 'sp')}
        self.sem = {e: es.enter_context(nc.semaphore("s_" + e)) for e in ('pe', 'act', 'dve', 'pool')}
        self.cnt = {e: 0 for e in self.sem}
        self.dsem = {qn: [es.enter_context(nc.semaphore("d_%s%d" % (qn, i))) for i in range(NDS)]
                     for qn in ('sp', 'pool')}
        self.dcnt = {qn: [0] * NDS for qn in self.dsem}
        self.drr = {qn: 0 for qn in self.dsem}
        self.waited = {e: {} for e in self.q}
        self.lastw = {}
        self.readers = {}
        self.nuniq = 0
        self.rot = {}

    def sb(self, name, shape, dt):
        return self.es.enter_context(self.nc.sbuf_tensor("sb_" + name, list(shape), dt))

    def ps(self, name, shape, dt=F32):
        return self.es.enter_context(self.nc.psum_tensor("pp_" + name, list(shape), dt))

    def rr(self, name, n):
        i = self.rot.get(name, 0)
        self.rot[name] = (i + 1) % n
        return i

    def _collect(self, eng, reads, writes):
        deps = []
        for k in reads:
            t = self.lastw.get(k)
            if t is not None:
                deps.append(t)
        for k in writes:
            t = self.lastw.get(k)
            if t is not None:
                deps.append(t)
            deps.extend(self.readers.get(k, {}).values())
        waits = []
        w = self.waited[eng]
        for (sem, val, e2) in deps:
            if e2 == eng and eng == 'pe':
                continue
            if w.get(id(sem), 0) < val:
                w[id(sem)] = val
                waits.append((sem, val))
        return waits

    def _commit(self, tok, rkey, reads, writes):
        for k in writes:
            self.lastw[k] = tok
            self.readers[k] = {}
        for k in reads:
            self.readers.setdefault(k, {})[rkey] = tok

    def op(self, eng, fn, reads=(), writes=()):
        waits = self._collect(eng, reads, writes)
        self.cnt[eng] += 1
        sem = self.sem[eng]
        tok = (sem, self.cnt[eng], eng)
        self.q[eng].append((waits, fn, sem, 1))
        self._commit(tok, eng, reads, writes)

    def dma(self, qn, out, in_, reads=(), writes=()):
        waits = self._collect(qn, reads, writes)
        j = self.drr[qn]
        self.drr[qn] = (j + 1) % NDS
        sem = self.dsem[qn][j]
        prev = 16 * self.dcnt[qn][j]
        w = self.waited[qn]
        if prev > 0 and w.get(id(sem), 0) < prev:
            w[id(sem)] = prev
            waits.append((sem, prev))
        self.dcnt[qn][j] += 1
        tok = (sem, 16 * self.dcnt[qn][j], 'dma')
        self.q[qn].append((waits, (lambda e: e.dma_start(out=out, in_=in_)), sem, 16))
        self.nuniq += 1
        self._commit(tok, ('dma', self.nuniq), reads, writes)

    def barrier(self):
        targets = [(self.sem[e], self.cnt[e]) for e in self.sem if self.cnt[e] > 0]
        for qn in self.dsem:
            for j in range(NDS):
                if self.dcnt[qn][j] > 0:
                    targets.append((self.dsem[qn][j], 16 * self.dcnt[qn][j]))
        for eng in ('pe', 'act', 'dve', 'pool', 'sp'):
            w = self.waited[eng]
            waits = []
            for (sem, val) in targets:
                if sem is self.sem.get(eng):
                    continue
                if w.get(id(sem), 0) < val:
                    w[id(sem)] = val
                    waits.append((sem, val))
            if waits:
                self.q[eng].append((waits, None, None, 0))

    def finish(self):
        waits = []
        for qn in self.dsem:
            for j in range(NDS):
                if self.dcnt[qn][j] > 0:
                    waits.append((self.dsem[qn][j], 16 * self.dcnt[qn][j]))
        for e in self.sem:
            if self.cnt[e] > 0:
                waits.append((self.sem[e], self.cnt[e]))
        self.q['sp'].append((waits, None, None, 0))

    def emit(self):
        nc = self.nc
        with nc.Block() as block:
            def mk(name):
                def body(eng):
                    for (waits, fn, sem, inc) in self.q[name]:
                        for (s, v) in waits:
                            eng.wait_ge(s, v)
                        if fn is not None:
                            fn(eng).then_inc(sem, inc)
                return body
            block.tensor(mk('pe'))
            block.scalar(mk('act'))
            block.vector(mk('dve'))
            block.gpsimd(mk('pool'))
            block.sync(mk('sp'))

    def mm(self, out, lhsT, rhs, start, stop, reads, writes):
        self.op('pe', lambda e: e.matmul(out, lhsT=lhsT, rhs=rhs, start=start, stop=stop), reads, writes)

    def tr(self, out, in_, ident, reads, writes):
        self.op('pe', lambda e: e.transpose(out=out, in_=in_, identity=ident), reads, writes)

    def act(self, out, in_, func, reads, writes, **kw):
        self.op('act', lambda e: e.activation(out=out, in_=in_, func=func, **kw), reads, writes)

    def tt(self, eng, out, in0, in1, op, reads, writes):
        self.op(eng, lambda e: e.tensor_tensor(out=out, in0=in0, in1=in1, op=op), reads, writes)

    def ts(self, eng, out, in0, s1, op0, reads, writes, s2=None, op1=None):
        if op1 is None:
            self.op(eng, lambda e: e.tensor_scalar(out=out, in0=in0, scalar1=s1, scalar2=None, op0=op0), reads, writes)
        else:
            self.op(eng, lambda e: e.tensor_scalar(out=out, in0=in0, scalar1=s1, scalar2=s2, op0=op0, op1=op1), reads, writes)

    def stt(self, out, in0, scalar, in1, op0, op1, reads, writes):
        self.op('dve', lambda e: e.scalar_tensor_tensor(out=out, in0=in0, scalar=scalar, in1=in1, op0=op0, op1=op1), reads, writes)

    def cp(self, eng, out, in_, reads, writes):
        if eng == 'act':
            self.op('act', lambda e: e.activation(out=out, in_=in_, func=AF.Copy), reads, writes)
        else:
            self.op(eng, lambda e: e.tensor_copy(out=out, in_=in_), reads, writes)

    def red(self, out, in_, reads, writes, op=ALU.add):
        self.op('dve', lambda e: e.tensor_reduce(out=out, in_=in_, axis=AX.X, op=op), reads, writes)

    def recip(self, out, in_, reads, writes):
        self.op('dve', lambda e: e.reciprocal(out=out, in_=in_), reads, writes)

    def memset(self, eng, ap, val, writes):
        self.op(eng, lambda e: e.memset(ap, val), (), writes)

    def asel(self, out, in_, pattern, cmp, fill, base, cm, reads, writes):
        self.op('pool', lambda e: e.affine_select(out=out, in_=in_, pattern=pattern, compare_op=cmp, fill=fill,
                                                  base=base, channel_multiplier=cm), reads, writes)


class Ctx:
    def __init__(self, nc, es):
        self.nc = nc
        self.P = P = Prog(nc, es)
        self.psf = [P.ps("psf%d" % i, [128, 512], F32) for i in range(6)]
        self.psb = [P.ps("psb%d" % i, [128, 1024], BF16) for i in range(2)]
        self.ident = P.sb("ident", [128, 128], BF16)
        P.memset('pool', self.ident[:], 1.0, ['ident'])
        P.asel(self.ident[:], self.ident[:], [[-1, 128]], ALU.is_equal, 0.0, 0, 1, ['ident'], ['ident'])

    def psum(self):
        i = self.P.rr('psf', 6)
        return self.psf[i], ('psf', i)

    def psumb(self):
        i = self.P.rr('psb', 2)
        return self.psb[i], ('psb', i)


def _din(nc, name, shape, dt=F32):
    return nc.dram_tensor(name, list(shape), dt, kind="ExternalInput").ap()


def _dout(nc, name, shape, dt=F32):
    return nc.dram_tensor(name, list(shape), dt, kind="ExternalOutput").ap()


def build_ac(lc, la):
    nc = bass.Bass("TRN2", target_bir_lowering=False)
    h_in = _din(nc, "h", [TOK, D])
    I = {}
    O = {}
    if lc is not None:
        for nm, shp in (("norm_ffn", [1, D]), ("ffn_w_in", [D, 2 * DFF]), ("ffn_w_out", [DFF, D]),
                        ("ple_norm", [1, D]), ("ple_w_gate", [D, D]), ("ple_w_proj", [256, D]), ("pT", [256, TOK])):
            I[nm] = _din(nc, nm, shp)
        if lc == 0:
            I["of"] = _din(nc, "of", [TOK, D]); I["ob"] = _din(nc, "ob", [TOK, D])
            I["sr"] = _din(nc, "sr", [TOK, D], BF16); I["g_rep"] = _din(nc, "g_rep", [1, D])
            I["w_out"] = _din(nc, "w_out", [D, D])
        elif lc == 1:
            I["on"] = _din(nc, "on", [TOK, D]); I["g_rep"] = _din(nc, "g_rep", [1, D])
            I["w_out"] = _din(nc, "w_out", [D, D])
        elif lc == 2:
            I["yf"] = _din(nc, "yf", [TOK, 2048]); I["yb"] = _din(nc, "yb", [TOK, 2048])
            I["sz"] = _din(nc, "sz", [TOK, 2048], BF16); I["g_rep"] = _din(nc, "g_rep", [1, 2048])
            I["w_out"] = _din(nc, "w_out", [2048, D])
        elif lc == 3:
            I["oa"] = _din(nc, "oa", [TOK, 3 * 16 * 65]); I["w_out"] = _din(nc, "w_out", [D, D])
    if la is not None:
        I["norm_mix"] = _din(nc, "norm_mix", [1, D])
        O["h_out"] = _dout(nc, "h_out", [TOK, D])
        if la == 0:
            I["w_in"] = _din(nc, "w_in", [D, 3104])
            I["wgf"] = _din(nc, "wgf", [16, 512]); I["wgb"] = _din(nc, "wgb", [16, 512])
            I["bgf"] = _din(nc, "bgf", [1, 512]); I["bgb"] = _din(nc, "bgb", [1, 512])
            O["q"] = _dout(nc, "q", [TOK, 512]); O["k"] = _dout(nc, "k", [TOK, 512])
            O["v"] = _dout(nc, "v", [TOK, 1024], BF16); O["sr"] = _dout(nc, "sro", [TOK, 1024], BF16)
            O["laf"] = _dout(nc, "laf", [TOK, 512]); O["lab"] = _dout(nc, "lab", [TOK, 512])
        elif la == 1:
            I["w_in"] = _din(nc, "w_in", [D, 3072])
            I["gq_rep"] = _din(nc, "gq_rep", [1, 1024]); I["gk_rep"] = _din(nc, "gk_rep", [1, 1024])
            I["rope"] = _din(nc, "rope", [TOK, 128])
            O["q"] = _dout(nc, "q", [TOK, 1024], BF16); O["k"] = _dout(nc, "k", [TOK, 1024], BF16)
            O["v"] = _dout(nc, "v", [TOK, 1024], BF16)
        elif la == 2:
            I["w_in"] = _din(nc, "w_in", [D, 6208]); I["dtb"] = _din(nc, "dtb", [1, 64])
            O["sz"] = _dout(nc, "szo", [TOK, 2048], BF16); O["xbc"] = _dout(nc, "xbc", [TOK, 4096], BF16)
            O["dt"] = _dout(nc, "dt", [TOK, 64])
        elif la == 3:
            I["w_in"] = _din(nc, "w_in", [D, 9216])
            I["gq_rep"] = _din(nc, "gq_rep", [1, 3072]); I["gk_rep"] = _din(nc, "gk_rep", [1, 3072])
            I["rope"] = _din(nc, "rope", [TOK, 128])
            O["q"] = _dout(nc, "q", [TOK, 3072], BF16); O["k"] = _dout(nc, "k", [TOK, 3072], BF16)
            O["v"] = _dout(nc, "v", [TOK, 3072], BF16)
    else:
        O["out"] = _dout(nc, "out", [TOK, D])

    with ExitStack() as es:
        C = Ctx(nc, es)
        P = C.P
        h = P.sb("h", [128, NT, D], F32)
        xT = P.sb("xT", [128, 8, TOK], BF16)
        actT = P.sb("actT", [128, 4, TOK], BF16)
        NW = 4
        wsl = [P.sb("w%d" % i, [128, 8, 512], BF16) for i in range(NW)]
        gt = [P.sb("gt%d" % i, [128, D], F32) for i in range(2)]
        scr = [P.sb("scr%d" % i, [128, D], F32) for i in range(3)]
        hnb = [P.sb("hnb%d" % i, [128, D], BF16) for i in range(2)]
        stg = [P.sb("stg%d" % i, [128, 512], F32) for i in range(4)]
        sml = [P.sb("sml%d" % i, [128, 64], F32) for i in range(4)]
        ptl = [P.sb("ptl%d" % i, [128, 2, 128], BF16) for i in range(2)]
        ident = C.ident

        def scratch():
            i = P.rr('scr', 3)
            return scr[i], ('scr', i)

        def staging():
            i = P.rr('stg', 4)
            return stg[i], ('stg', i)

        def small():
            i = P.rr('sml', 4)
            return sml[i], ('sml', i)

        def wslot():
            i = P.rr('w', NW)
            return wsl[i], ('w', i)

        hv = h_in.rearrange("(t p) d -> p t d", p=128)
        for t4 in range(4):
            P.dma('sp', h[:, 4 * t4:4 * t4 + 4, :], hv[:, 4 * t4:4 * t4 + 4, :], (), [('h', t) for t in range(4 * t4, 4 * t4 + 4)])

        def load_gain(slot, ap, n=D, off=0):
            P.dma('sp', gt[slot][:, 0:n], ap[0:1, off:off + n].partition_broadcast(128), (), [('gt', slot)])

        def load_w(W, k0, kc, n0, nw):
            wt, wk = wslot()
            src = W[k0 * 128:(k0 + kc) * 128, n0:n0 + nw].rearrange("(c p) n -> p c n", p=128)
            P.dma('pool', wt[:, 0:kc, 0:nw], src, (), [wk])
            return wt, wk

        def rstd_from_ss(ss_ap, ss_key, n_over, cols):
            P.act(ss_ap, ss_ap, AF.Sqrt, [ss_key], [ss_key], scale=1.0 / n_over, bias=EPS)
            P.recip(ss_ap, ss_ap, [ss_key], [ss_key])

        def transpose_to_xT(src_bf, src_key, t, nchunk=8):
            pb, pbk = C.psumb()
            for c in range(nchunk):
                P.tr(pb[:, c * 128:(c + 1) * 128], src_bf[:, c * 128:(c + 1) * 128], ident[:], [src_key, 'ident'], [pbk])
            eng = 'act' if (t % 2 == 0) else 'dve'
            P.cp(eng, xT[:, 0:nchunk, t * 128:(t + 1) * 128], pb[:, 0:nchunk * 128].rearrange("p (c n) -> p c n", n=128),
                 [pbk], [('xT', t)])

        def norm_T(gslot):
            for t in range(NT):
                sq, sqk = scratch()
                sm, smk = small()
                P.act(sq[:], h[:, t, :], AF.Square, [('h', t)], [sqk, smk], accum_out=sm[:, 0:1])
                rstd_from_ss(sm[:, 0:1], smk, D, 1)
                i = P.rr('hnb', 2)
                P.stt(hnb[i][:], h[:, t, :], sm[:, 0:1], gt[gslot][:], ALU.mult, ALU.mult,
                      [('h', t), smk, ('gt', gslot)], [('hnb', i)])
                transpose_to_xT(hnb[i], ('hnb', i), t)

        def linear_tm(W, kc, n0, n1, epilogue, k0=0, xsrc=None, xkeys=None):
            xs = xT if xsrc is None else xsrc
            for nb in range(n0, n1, 512):
                nw = min(512, n1 - nb)
                wt, wk = load_w(W, k0, kc, nb, nw)
                for t in range(NT):
                    ps, pk = C.psum()
                    for c in range(kc):
                        P.mm(ps[:, 0:nw], xs[:, c, t * 128:(t + 1) * 128], wt[:, c, 0:nw], c == 0, c == kc - 1,
                             [('xT', t), wk], [pk])
                    epilogue(t, nb, nw, ps, pk)

        def add_to_h(t, nb, nw, ps, pk):
            P.tt('dve', h[:, t, nb:nb + nw], h[:, t, nb:nb + nw], ps[:, 0:nw], ALU.add, [('h', t), pk], [('h', t)])

        def store_rows(dst, t, c0, cw, src_ap, src_key):
            P.dma('sp', dst[t * 128:(t + 1) * 128, c0:c0 + cw], src_ap, [src_key], ())

        if lc is not None:
            if lc == 0:
                load_gain(0, I["g_rep"])
                for t in range(NT):
                    a, ak = scratch(); b, bk = scratch()
                    P.dma('sp', a[:], I["of"][t * 128:(t + 1) * 128, :], (), [ak])
                    P.dma('sp', b[:], I["ob"][t * 128:(t + 1) * 128, :], (), [bk])
                    i = P.rr('hnb', 2)
                    P.dma('sp', hnb[i][:], I["sr"][t * 128:(t + 1) * 128, :], (), [('hnb', i)])
                    P.tt('dve', a[:], a[:], b[:], ALU.add, [ak, bk], [ak])
                    sm, smk = small()
                    P.act(b[:], a[:], AF.Square, [ak], [bk])
                    P.red(sm[:, 0:4], b[:].rearrange("p (g n) -> p g n", n=256), [bk], [smk])
                    rstd_from_ss(sm[:, 0:4], smk, 256, 4)
                    P.tt('dve', a[:].rearrange("p (g n) -> p g n", n=256), a[:].rearrange("p (g n) -> p g n", n=256),
                         sm[:, 0:4].unsqueeze(2).to_broadcast([128, 4, 256]), ALU.mult, [ak, smk], [ak])
                    P.tt('pool', a[:], a[:], gt[0][:], ALU.mult, [ak, ('gt', 0)], [ak])
                    P.tt('dve', hnb[i][:], a[:], hnb[i][:], ALU.mult, [ak, ('hnb', i)], [('hnb', i)])
                    transpose_to_xT(hnb[i], ('hnb', i), t)
                linear_tm(I["w_out"], 8, 0, D, add_to_h)
            elif lc == 1:
                load_gain(0, I["g_rep"])
                lam_init = 0.8 - 0.6 * math.exp(-0.3 * 1)
                for t in range(NT):
                    a, ak = scratch(); b, bk = scratch()
                    P.dma('sp', a[:], I["on"][t * 128:(t + 1) * 128, :], (), [ak])
                    sm, smk = small()
                    P.act(b[:], a[:], AF.Square, [ak], [bk])
                    P.red(sm[:, 0:8], b[:].rearrange("p (g n) -> p g n", n=128), [bk], [smk])
                    rstd_from_ss(sm[:, 0:8], smk, 128, 8)
                    P.tt('dve', a[:].rearrange("p (g n) -> p g n", n=128), a[:].rearrange("p (g n) -> p g n", n=128),
                         sm[:, 0:8].unsqueeze(2).to_broadcast([128, 8, 128]), ALU.mult, [ak, smk], [ak])
                    i = P.rr('hnb', 2)
                    P.stt(hnb[i][:], a[:], 1.0 - lam_init, gt[0][:], ALU.mult, ALU.mult, [ak, ('gt', 0)], [('hnb', i)])
                    transpose_to_xT(hnb[i], ('hnb', i), t)
                linear_tm(I["w_out"], 8, 0, D, add_to_h)
            elif lc == 2:
                for half in range(2):
                    load_gain(0, I["g_rep"], D, half * D)
                    for t in range(NT):
                        a, ak = scratch(); b, bk = scratch()
                        P.dma('sp', a[:], I["yf"][t * 128:(t + 1) * 128, half * D:(half + 1) * D], (), [ak])
                        P.dma('sp', b[:], I["yb"][t * 128:(t + 1) * 128, half * D:(half + 1) * D], (), [bk])
                        i = P.rr('hnb', 2)
                        P.dma('sp', hnb[i][:], I["sz"][t * 128:(t + 1) * 128, half * D:(half + 1) * D], (), [('hnb', i)])
                        P.tt('dve', a[:], a[:], b[:], ALU.add, [ak, bk], [ak])
                        P.tt('dve', a[:], a[:], hnb[i][:], ALU.mult, [ak, ('hnb', i)], [ak])
                        sm, smk = small()
                        P.act(b[:], a[:], AF.Square, [ak], [bk])
                        P.red(sm[:, 0:4], b[:].rearrange("p (g n) -> p g n", n=256), [bk], [smk])
                        rstd_from_ss(sm[:, 0:4], smk, 256, 4)
                        P.tt('dve', a[:].rearrange("p (g n) -> p g n", n=256), a[:].rearrange("p (g n) -> p g n", n=256),
                             sm[:, 0:4].unsqueeze(2).to_broadcast([128, 4, 256]), ALU.mult, [ak, smk], [ak])
                        P.tt('dve', hnb[i][:], a[:], gt[0][:], ALU.mult, [ak, ('gt', 0), ('hnb', i)], [('hnb', i)])
                        transpose_to_xT(hnb[i], ('hnb', i), t)
                    linear_tm(I["w_out"], 8, 0, D, add_to_h, k0=8 * half)
            elif lc == 3:
                oav = I["oa"].rearrange("n (g x) -> n g x", g=3)
                oat = P.sb("oat", [128, 3, 1040], F32)
                for t in range(NT):
                    P.dma('sp', oat[:], oav[t * 128:(t + 1) * 128, :, :], (), ['oat'])
                    P.tt('dve', oat[:, 0, :], oat[:, 0, :], oat[:, 1, :], ALU.add, ['oat'], ['oat'])
                    P.tt('dve', oat[:, 0, :], oat[:, 0, :], oat[:, 2, :], ALU.add, ['oat'], ['oat'])
                    sv = oat[:, 0, :].rearrange("p (h x) -> p h x", x=65)
                    sm, smk = small()
                    P.recip(sm[:, 0:16], sv[:, :, 64], ['oat'], [smk])
                    i = P.rr('hnb', 2)
                    P.tt('dve', hnb[i][:].rearrange("p (h x) -> p h x", x=64), sv[:, :, 0:64],
                         sm[:, 0:16].unsqueeze(2).to_broadcast([128, 16, 64]), ALU.mult, ['oat', smk], [('hnb', i)])
                    transpose_to_xT(hnb[i], ('hnb', i), t)
                linear_tm(I["w_out"], 8, 0, D, add_to_h)
            load_gain(1, I["norm_ffn"])
            norm_T(1)
            NCH = DFF // 128
            for g0 in range(0, NCH, 4):
                gc = min(4, NCH - g0)
                fw = gc * 128
                wg, wgk = load_w(I["ffn_w_in"], 0, 8, g0 * 128, fw)
                wu, wuk = load_w(I["ffn_w_in"], 0, 8, DFF + g0 * 128, fw)
                wo, wok = wslot()
                wov = wo[:].rearrange("p a n -> p (a n)")
                P.dma('pool', wov[:, 0:gc * 1024].rearrange("p (c n) -> p c n", n=1024),
                      I["ffn_w_out"][g0 * 128:(g0 + gc) * 128, :].rearrange("(c p) n -> p c n", p=128), (), [wok])
                for c in range(gc):
                    for tb in range(4):
                        pg, pgk = C.psum(); pu, puk = C.psum()
                        xk = [('xT', t) for t in range(4 * tb, 4 * tb + 4)]
                        for k in range(8):
                            P.mm(pg[:], wg[:, k, c * 128:(c + 1) * 128], xT[:, k, tb * 512:(tb + 1) * 512], k == 0, k == 7, xk + [wgk], [pgk])
                        for k in range(8):
                            P.mm(pu[:], wu[:, k, c * 128:(c + 1) * 128], xT[:, k, tb * 512:(tb + 1) * 512], k == 0, k == 7, xk + [wuk], [puk])
                        sg, sgk = staging()
                        P.act(sg[:], pg[:], AF.Silu, [pgk], [sgk])
                        P.tt('dve', actT[:, c, tb * 512:(tb + 1) * 512], sg[:], pu[:], ALU.mult, [sgk, puk], [('actT', c, tb)])
                for t in range(NT):
                    for nh in range(2):
                        py, pyk = C.psum()
                        for c in range(gc):
                            P.mm(py[:], actT[:, c, t * 128:(t + 1) * 128], wov[:, c * 1024 + nh * 512:c * 1024 + nh * 512 + 512],
                                 c == 0, c == gc - 1, [('actT', c, t // 4), wok], [pyk])
                        add_to_h(t, nh * 512, 512, py, pyk)
            load_gain(0, I["ple_norm"])
            norm_T(0)
            pTv = I["pT"]
            for nh in range(2):
                wgt, wgtk = load_w(I["ple_w_gate"], 0, 8, nh * 512, 512)
                wpj, wpjk = load_w(I["ple_w_proj"], 0, 2, nh * 512, 512)
                for t in range(NT):
                    i = P.rr('ptl', 2)
                    P.dma('pool', ptl[i][:], pTv[:, t * 128:(t + 1) * 128].rearrange("(c p) n -> p c n", p=128), (), [('ptl', i)])
                    pga, pgak = C.psum(); ppr, pprk = C.psum()
                    for k in range(8):
                        P.mm(pga[:], xT[:, k, t * 128:(t + 1) * 128], wgt[:, k, :], k == 0, k == 7, [('xT', t), wgtk], [pgak])
                    for k in range(2):
                        P.mm(ppr[:], ptl[i][:, k, :], wpj[:, k, :], k == 0, k == 1, [('ptl', i), wpjk], [pprk])
                    sg, sgk = staging()
                    P.act(sg[:], pga[:], AF.Sigmoid, [pgak], [sgk])
                    P.tt('dve', sg[:], sg[:], ppr[:], ALU.mult, [sgk, pprk], [sgk])
                    P.tt('pool', h[:, t, nh * 512:(nh + 1) * 512], h[:, t, nh * 512:(nh + 1) * 512], sg[:], ALU.add,
                         [('h', t), sgk], [('h', t)])

        if la is None:
            ov = O["out"].rearrange("(t p) d -> p t d", p=128)
            for t4 in range(4):
                P.dma('sp', ov[:, 4 * t4:4 * t4 + 4, :], h[:, 4 * t4:4 * t4 + 4, :], [('h', t) for t in range(4 * t4, 4 * t4 + 4)], ())
        else:
            ov = O["h_out"].rearrange("(t p) d -> p t d", p=128)
            for t4 in range(4):
                P.dma('sp', ov[:, 4 * t4:4 * t4 + 4, :], h[:, 4 * t4:4 * t4 + 4, :], [('h', t) for t in range(4 * t4, 4 * t4 + 4)], ())
            load_gain(1, I["norm_mix"])
            norm_T(1)
            W = I["w_in"]

            def ep_copy(dst, c_off, bf, func=AF.Copy, scale=None):
                def ep(t, nb, nw, ps, pk):
                    s, sk = staging()
                    sv = s[:].bitcast(BF16)[:, 0:nw] if bf else s[:, 0:nw]
                    kw = {} if scale is None else {"scale": scale}
                    P.act(sv, ps[:, 0:nw], func, [pk], [sk], **kw)
                    store_rows(dst, t, nb - c_off, nw, sv, sk)
                return ep

            def make_ep_normrope(dst, c_off, gslot, hd=64):
                def ep(t, nb, nw, ps, pk):
                    ns = nw // hd
                    a, ak = scratch(); b, bk = scratch()
                    sm, smk = small()
                    P.act(a[:, 0:nw], ps[:, 0:nw], AF.Square, [pk], [ak])
                    P.red(sm[:, 0:ns], a[:, 0:nw].rearrange("p (g n) -> p g n", n=hd), [ak], [smk])
                    rstd_from_ss(sm[:, 0:ns], smk, hd, ns)
                    P.tt('dve', a[:, 0:nw].rearrange("p (g n) -> p g n", n=hd), ps[:, 0:nw].rearrange("p (g n) -> p g n", n=hd),
                         sm[:, 0:ns].unsqueeze(2).to_broadcast([128, ns, hd]), ALU.mult, [pk, smk], [ak])
                    goff = nb - c_off
                    P.tt('pool', a[:, 0:nw], a[:, 0:nw], gt[gslot][:, goff % D:goff % D + nw], ALU.mult, [ak, ('gt', gslot)], [ak])
                    rp = ropet[t % 2]
                    rk = ('rope', t % 2)
                    av = a[:, 0:nw].rearrange("p (g two n) -> p g two n", two=2, n=hd // 2)
                    bv = b[:, 0:nw].rearrange("p (g two n) -> p g two n", two=2, n=hd // 2)
                    sinv = rp[:, hd:2 * hd].rearrange("p (two n) -> p two n", two=2)
                    P.tt('pool', bv[:, :, 0, :], av[:, :, 1, :], sinv[:, 0:1, :].to_broadcast([128, ns, hd // 2]), ALU.mult, [ak, rk], [bk])
                    P.tt('pool', bv[:, :, 1, :], av[:, :, 0, :], sinv[:, 1:2, :].to_broadcast([128, ns, hd // 2]), ALU.mult, [ak, rk], [bk])
                    P.tt('dve', a[:, 0:nw].rearrange("p (g n) -> p g n", n=hd), a[:, 0:nw].rearrange("p (g n) -> p g n", n=hd),
                         rp[:, 0:hd].unsqueeze(1).to_broadcast([128, ns, hd]), ALU.mult, [ak, rk], [ak])
                    s, sk = staging()
                    sv = s[:].bitcast(BF16)[:, 0:nw]
                    P.tt('dve', sv, a[:, 0:nw], b[:, 0:nw], ALU.add, [ak, bk], [sk])
                    store_rows(dst, t, nb - c_off, nw, sv, sk)
                return ep

            if la in (1, 3):
                ropet = [P.sb("rope%d" % i, [128, 128], F32) for i in range(2)]

            def linear_tm_rope(W, n0, n1, epilogue):
                for nb in range(n0, n1, 512):
                    nw = min(512, n1 - nb)
                    wt, wk = load_w(W, 0, 8, nb, nw)
                    for t in range(NT):
                        ps, pk = C.psum()
                        for c in range(8):
                            P.mm(ps[:, 0:nw], xT[:, c, t * 128:(t + 1) * 128], wt[:, c, 0:nw], c == 0, c == 7, [('xT', t), wk], [pk])
                        P.dma('sp', ropet[t % 2][:], I["rope"][t * 128:(t + 1) * 128, :], (), [('rope', t % 2)])
                        epilogue(t, nb, nw, ps, pk)

            if la == 0:
                linear_tm(W, 8, 0, 512, ep_copy(O["q"], 0, False, scale=128 ** -0.5))
                linear_tm(W, 8, 512, 1024, ep_copy(O["k"], 512, False))
                linear_tm(W, 8, 1024, 2048, ep_copy(O["v"], 1024, True))
                linear_tm(W, 8, 2048, 3072, ep_copy(O["sr"], 2048, True, func=AF.Silu))
                zT = [P.sb("zT%d" % i, [16, TOK], BF16) for i in range(2)]
                wgs = [P.sb("wgs%d" % i, [16, 512], BF16) for i in range(2)]
                bgs = [P.sb("bgs%d" % i, [1, 512], BF16) for i in range(2)]
                ones = P.sb("ones", [1, 128], BF16)
                P.memset('pool', ones[:], 1.0, ['ones'])
                wz, wzk = load_w(W, 0, 8, 3072, 32)
                for i, (wn, bn) in enumerate((("wgf", "bgf"), ("wgb", "bgb"))):
                    P.dma('pool', wgs[i][:], I[wn][:, :], (), [('wgs', i)])
                    P.dma('pool', bgs[i][:], I[bn][:, :], (), [('bgs', i)])
                for i in range(2):
                    for tb in range(4):
                        ps, pk = C.psum()
                        for k in range(8):
                            P.mm(ps[0:16, :], wz[:, k, 16 * i:16 * i + 16], xT[:, k, tb * 512:(tb + 1) * 512], k == 0, k == 7,
                                 [('xT', t) for t in range(4 * tb, 4 * tb + 4)] + [wzk], [pk])
                        P.cp('act', zT[i][:, tb * 512:(tb + 1) * 512], ps[0:16, :], [pk], [('zT', i, tb)])
                for i, dst in enumerate((O["laf"], O["lab"])):
                    for t in range(NT):
                        ps, pk = C.psum()
                        P.mm(ps[:], zT[i][:, t * 128:(t + 1) * 128], wgs[i][:], True, False, [('zT', i, t // 4), ('wgs', i)], [pk])
                        P.mm(ps[:], ones[:], bgs[i][:], False, True, ['ones', ('bgs', i)], [pk])
                        s, sk = staging()
                        P.act(s[:], ps[:], AF.Exp, [pk], [sk], scale=-1.0)
                        P.act(s[:], s[:], AF.Ln, [sk], [sk], bias=1.0)
                        P.ts('dve', s[:], s[:], -1.0 / 16.0, ALU.mult, [sk], [sk])
                        store_rows(dst, t, 0, 512, s[:], sk)
            elif la == 1:
                load_gain(0, I["gq_rep"])
                linear_tm_rope(W, 0, 1024, make_ep_normrope(O["q"], 0, 0))
                load_gain(0, I["gk_rep"])
                linear_tm_rope(W, 1024, 2048, make_ep_normrope(O["k"], 1024, 0))
                linear_tm(W, 8, 2048, 3072, ep_copy(O["v"], 2048, True))
            elif la == 2:
                linear_tm(W, 8, 0, 2048, ep_copy(O["sz"], 0, True, func=AF.Silu))
                linear_tm(W, 8, 2048, 6144, ep_copy(O["xbc"], 2048, True))
                dtb = P.sb("dtb", [128, 64], F32)
                P.dma('sp', dtb[:], I["dtb"][0:1, :].partition_broadcast(128), (), ['dtb'])

                def ep_dt(t, nb, nw, ps, pk):
                    s, sk = staging()
                    P.tt('dve', s[:, 0:64], ps[:, 0:64], dtb[:], ALU.add, [pk, 'dtb'], [sk])
                    P.act(s[:, 0:64], s[:, 0:64], AF.Exp, [sk], [sk])
                    P.act(s[:, 0:64], s[:, 0:64], AF.Ln, [sk], [sk], bias=1.0)
                    store_rows(O["dt"], t, 0, 64, s[:, 0:64], sk)
                linear_tm(W, 8, 6144, 6208, ep_dt)
            elif la == 3:
                for part, dst, gn in ((0, O["q"], "gq_rep"), (1, O["k"], "gk_rep")):
                    for g in range(3):
                        load_gain(0, I[gn], D, g * D)
                        linear_tm_rope(W, part * 3072 + g * D, part * 3072 + (g + 1) * D,
                                       make_ep_normrope(dst, part * 3072, 0))
                linear_tm(W, 8, 6144, 9216, ep_copy(O["v"], 6144, True))
        P.finish()
        P.emit()
    return nc


def build_tri_masks(C, chunk=None):
    P = C.P
    M = {}
    for nm, cmp, base, cm, step in (("le", ALU.is_ge, 0, -1, 1), ("gt", ALU.is_gt, 0, 1, -1),
                                    ("ge", ALU.is_ge, 0, 1, -1), ("lt", ALU.is_gt, 0, -1, 1)):
        m = P.sb("mask_" + nm, [128, 128], F32)
        k = 'mask_' + nm
        P.memset('pool', m[:], 1.0, [k])
        P.asel(m[:], m[:], [[step, 128]], cmp, 0.0, base, cm, [k], [k])
        if chunk == 64:
            P.memset('pool', m[0:64, 64:128], 0.0, [k])
            P.memset('pool', m[64:128, 0:64], 0.0, [k])
        M[nm] = (m, k)
    return M


def build_b0():
    nc = bass.Bass("TRN2", target_bir_lowering=False)
    qT = _din(nc, "qT", [128, SEQ]); kT = _din(nc, "kT", [128, SEQ])
    kk = _din(nc, "k", [SEQ, 128]); vv = _din(nc, "v", [SEQ, 256], BF16)
    la = {"f": _din(nc, "laf", [SEQ, 128]), "b": _din(nc, "lab", [SEQ, 128])}
    oo = {"f": _dout(nc, "of", [SEQ, 256]), "b": _dout(nc, "ob", [SEQ, 256])}
    with ExitStack() as es:
        C = Ctx(nc, es)
        P = C.P
        M = build_tri_masks(C, 64)
        NB = 3
        T = {}
        for d in "fb":
            T[d] = dict(
                qT=[P.sb("qT%s%d" % (d, i), [128, 128], F32) for i in range(NB)],
                kT=[P.sb("kT%s%d" % (d, i), [128, 128], F32) for i in range(NB)],
                k=[P.sb("k%s%d" % (d, i), [128, 128], F32) for i in range(NB)],
                v=[P.sb("v%s%d" % (d, i), [128, 256], BF16) for i in range(NB)],
                la=[P.sb("la%s%d" % (d, i), [128, 128], F32) for i in range(NB)],
                E=[P.sb("E%s%d" % (d, i), [128, 128], F32) for i in range(2)],
                Ei=[P.sb("Ei%s%d" % (d, i), [128, 128], F32) for i in range(2)],
                Es=[P.sb("Es%s%d" % (d, i), [128, 128], F32) for i in range(2)],
                qin=[P.sb("qin%s%d" % (d, i), [128, 128], BF16) for i in range(2)],
                kout=[P.sb("kout%s%d" % (d, i), [128, 128], BF16) for i in range(2)],
                kst=[P.sb("kst%s%d" % (d, i), [128, 128], BF16) for i in range(2)],
                sc=[P.sb("sc%s%d" % (d, i), [128, 128], BF16) for i in range(2)],
                o=[P.sb("o%s%d" % (d, i), [128, 256], F32) for i in range(2)],
                S=P.sb("S%s" % d, [128, 256], F32),
                Sb=[P.sb("Sb%s%d" % (d, i), [128, 256], BF16) for i in range(2)],
            )
            P.memset('pool', T[d]["S"][:], 0.0, [('S', d)])
            P.memset('pool', T[d]["Sb"][0][:], 0.0, [('Sb', d, 0)])
            T[d]["sbi"] = 0
        NTL = SEQ // 128
        for idx in range(NTL):
            for d in "fb":
                t = idx if d == "f" else NTL - 1 - idx
                R = T[d]
                i3 = P.rr('in' + d, NB)
                i2 = P.rr('w' + d, 2)
                kq, kkT, kkk, kv, kla = [(n, d, i3) for n in ("qT", "kT", "k", "v", "la")]
                P.dma('sp', R["qT"][i3][:], qT[:, t * 128:(t + 1) * 128], (), [kq])
                P.dma('sp', R["kT"][i3][:], kT[:, t * 128:(t + 1) * 128], (), [kkT])
                P.dma('sp', R["k"][i3][:], kk[t * 128:(t + 1) * 128, :], (), [kkk])
                P.dma('sp', R["v"][i3][:], vv[t * 128:(t + 1) * 128, :], (), [kv])
                P.dma('sp', R["la"][i3][:], la[d][t * 128:(t + 1) * 128, :], (), [kla])
                minc, minck = M["le"] if d == "f" else M["ge"]
                mexc, mexck = M["gt"] if d == "f" else M["lt"]
                msc, msck = M["le"] if d == "f" else M["gt"]
                pb, pbk = C.psum()
                P.mm(pb[:, 0:128], R["la"][i3][:], minc[:], True, True, [kla, minck], [pbk])
                psf, psfk = C.psum()
                P.mm(psf[:, 0:128], mexc[:], R["la"][i3][:], True, True, [kla, mexck], [psfk])
                E, Ei, Es = R["E"][i2], R["Ei"][i2], R["Es"][i2]
                kE, kEi, kEs = ('E', d, i2), ('Ei', d, i2), ('Es', d, i2)
                P.act(E[:], pb[:, 0:128], AF.Exp, [pbk], [kE])
                P.act(Ei[:], pb[:, 0:128], AF.Exp, [pbk], [kEi], scale=-1.0)
                P.act(Es[:], psf[:, 0:128], AF.Exp, [psfk], [kEs])
                qin, kout, kst, sc = R["qin"][i2], R["kout"][i2], R["kst"][i2], R["sc"][i2]
                kqin, kkout, kkst, ksc = ('qin', d, i2), ('kout', d, i2), ('kst', d, i2), ('sc', d, i2)
                P.tt('dve', qin[:], R["qT"][i3][:], E[:], ALU.mult, [kq, kE], [kqin])
                P.tt('pool', kout[:], R["kT"][i3][:], Ei[:], ALU.mult, [kkT, kEi], [kkout])
                P.tt('pool', kst[:], R["k"][i3][:], Es[:], ALU.mult, [kkk, kEs], [kkst])
                psc, psck = C.psum()
                P.mm(psc[:, 0:128], kout[:], qin[:], True, True, [kkout, kqin], [psck])
                P.tt('dve', sc[:], psc[:, 0:128], msc[:], ALU.mult, [psck, msck], [ksc])
                po, pok = C.psum()
                P.mm(po[:, 0:256], sc[:], R["v"][i3][:], True, False, [ksc, kv], [pok])
                order = (0, 1) if d == "f" else (1, 0)
                for ci, c in enumerate(order):
                    sbi = R["sbi"]
                    P.mm(po[64 * c:64 * c + 64, 0:256], qin[:, 64 * c:64 * c + 64], R["Sb"][sbi][:], False, ci == 1,
                         [kqin, ('Sb', d, sbi)], [pok])
                    pu, puk = C.psum()
                    P.mm(pu[:, 0:256], kst[64 * c:64 * c + 64, :], R["v"][i3][64 * c:64 * c + 64, :], True, True, [kkst, kv], [puk])
                    dcol = (64 * c + 63) if d == "f" else (64 * c)
                    P.stt(R["S"][:], R["S"][:], E[:, dcol:dcol + 1], pu[:, 0:256], ALU.mult, ALU.add,
                          [('S', d), kE, puk], [('S', d)])
                    nsb = 1 - sbi
                    P.cp('act', R["Sb"][nsb][:], R["S"][:], [('S', d)], [('Sb', d, nsb)])
                    R["sbi"] = nsb
                o, ok = R["o"][i2], ('o', d, i2)
                P.cp('act', o[:], po[:, 0:256], [pok], [ok])
                P.dma('sp', oo[d][t * 128:(t + 1) * 128, :], o[:], [ok], ())
        P.finish()
        P.emit()
    return nc


def _c(a):
    return np.ascontiguousarray(a)


def rope_table(hd=64):
    half = hd // 2
    inv = (10000.0 ** (-np.arange(half, dtype=np.float32) * 2.0 / hd)).astype(np.float32)
    ang = np.arange(SEQ, dtype=np.float32)[:, None] * inv[None, :]
    cos = np.cos(ang).astype(np.float32)
    sin = np.sin(ang).astype(np.float32)
    return _c(np.concatenate([cos, cos, -sin, sin], axis=1))


def tok_slice(c):
    b, q = c // 4, c % 4
    return b, slice(q * TOK, (q + 1) * TOK)


def common_c_inputs(W, lc, c):
    b, sl = tok_slice(c)
    return {"norm_ffn": W["norm_ffn"][lc:lc + 1], "ffn_w_in": W["ffn_w_in"][lc], "ffn_w_out": W["ffn_w_out"][lc],
            "ple_norm": W["ple_norm"][lc:lc + 1], "ple_w_gate": W["ple_w_gate"][lc], "ple_w_proj": W["ple_w_proj"][lc],
            "pT": _c(W["p"][lc, b, sl, :].T)}


def build_b1(lam_init, nunits=2, seq=SEQ):
    nc = bass.Bass("TRN2", target_bir_lowering=False)
    qT = _din(nc, "qT", [nunits * 128, seq], BF16)
    kT = _din(nc, "kT", [nunits * 128, seq], BF16)
    vv = _din(nc, "v", [nunits * seq, 128], BF16)
    lamv = _din(nc, "lamv", [1, 256])
    oo = _dout(nc, "o", [nunits * seq, 128])
    NKT = seq // 128
    NQB = seq // 512
    with ExitStack() as es:
        P = Prog(nc, es)
        pS = [P.ps("pS%d" % i, [128, 512], F32) for i in range(3)]
        pO = [P.ps("pO%d" % i, [128, 512], F32) for i in range(4)]
        pL = P.ps("pL", [128, 512], F32)
        qs = [P.sb("qs%d" % i, [128, seq], BF16) for i in range(2)]
        ks = [P.sb("ks%d" % i, [128, seq], BF16) for i in range(2)]
        vs = [P.sb("vs%d" % i, [128, NKT, 130], BF16) for i in range(2)]
        eT = [P.sb("eT%d" % i, [128, 512], BF16) for i in range(3)]
        ob = [P.sb("ob%d" % i, [128, 128], F32) for i in range(4)]
        o2 = [P.sb("o2%d" % i, [128, 128], F32) for i in range(2)]
        rl = [P.sb("rl%d" % i, [128, 4], F32) for i in range(4)]
        lt = P.sb("lt", [1, 256], F32)
        lw = P.sb("lw", [1, 8], F32)
        ones = P.sb("ones", [1, 128], F32)
        lamb = P.sb("lamb", [128, 2], F32)
        P.dma('sp', lt[:], lamv[:, :], (), ['lt'])
        P.memset('pool', ones[:], 1.0, ['ones'])
        P.tt('dve', lt[:, 0:64], lt[:, 0:64], lt[:, 64:128], ALU.mult, ['lt'], ['lt'])
        P.tt('dve', lt[:, 128:192], lt[:, 128:192], lt[:, 192:256], ALU.mult, ['lt'], ['lt'])
        P.red(lw[:, 0:1], lt[:, 0:64], ['lt'], ['lw'])
        P.red(lw[:, 1:2], lt[:, 128:192], ['lt'], ['lw'])
        P.act(lw[:, 0:2], lw[:, 0:2], AF.Exp, ['lw'], ['lw'])
        P.tt('dve', lw[:, 2:3], lw[:, 0:1], lw[:, 1:2], ALU.subtract, ['lw'], ['lw'])
        P.ts('dve', lw[:, 4:6], lw[:, 2:3].to_broadcast([1, 2]), lam_init, ALU.add, ['lw'], ['lw'])
        P.mm(pL[:, 0:2], ones[:], lw[:, 4:6], True, True, ['ones', 'lw'], ['pL'])
        P.cp('dve', lamb[:], pL[:, 0:2], ['pL'], ['lamb'])
        scale = 64 ** -0.5
        for u in range(nunits):
            bi = u % 2
            for cb in range(4):
                cs = slice(cb * (seq // 4), (cb + 1) * (seq // 4))
                P.dma('sp', qs[bi][:, cs], qT[u * 128:(u + 1) * 128, cs], (), [('qs', bi, cb)])
                P.dma('sp', ks[bi][:, cs], kT[u * 128:(u + 1) * 128, cs], (), [('ks', bi, cb)])
                kts = slice(cb * (NKT // 4), (cb + 1) * (NKT // 4))
                P.dma('sp', vs[bi][:, kts, 0:128],
                      vv[u * seq + cb * (seq // 4):u * seq + (cb + 1) * (seq // 4), :].rearrange("(t p) d -> p t d", p=128),
                      (), [('vs', bi, cb)])
            P.memset('pool', vs[bi][:, :, 128:130], 1.0, [('vones', bi)])
            steps = [(qb, kt, s) for qb in range(NQB) for kt in range(NKT) for s in range(2)]

            def emit_S(i):
                qb, kt, s = steps[i]
                j = i % 3
                P.mm(pS[j][:], ks[bi][64 * s:64 * s + 64, kt * 128:(kt + 1) * 128], qs[bi][64 * s:64 * s + 64, qb * 512:(qb + 1) * 512],
                     True, True, [('ks', bi, kt // (NKT // 4)), ('qs', bi, qb // (NQB // 4))], [('pS', j)])
                P.act(eT[j][:], pS[j][:], AF.Exp, [('pS', j)], [('eT', j)], scale=scale)

            def emit_AV(i):
                qb, kt, s = steps[i]
                j = i % 3
                for jq in range(4):
                    bank = s * 2 + jq // 2
                    col = (jq % 2) * 256
                    P.mm(pO[bank][:, col:col + 129], eT[j][:, jq * 128:(jq + 1) * 128], vs[bi][:, kt, 0:129],
                         (kt == 0 and jq % 2 == 0), (kt == NKT - 1),
                         [('eT', j), ('vs', bi, kt // (NKT // 4)), ('vones', bi)], [('pO', bank)])
                if kt == NKT - 1 and s == 1:
                    for jq in range(4):
                        b0, b1 = pO[jq // 2], pO[2 + jq // 2]
                        k0, k1 = ('pO', jq // 2), ('pO', 2 + jq // 2)
                        col = (jq % 2) * 256
                        r = rl[P.rr('rl', 4)]
                        rk = ('rl', id(r))
                        P.recip(r[:, 0:1], b0[:, col + 128:col + 129], [k0], [rk])
                        P.recip(r[:, 1:2], b1[:, col + 128:col + 129], [k1], [rk])
                        P.tt('dve', r[:, 2:3], r[:, 1:2], lamb[:, 0:1], ALU.mult, [rk, 'lamb'], [rk])
                        oi = P.rr('ob', 4)
                        o2i = P.rr('o2', 2)
                        P.ts('dve', ob[oi][:], b0[:, col:col + 128], r[:, 0:1], ALU.mult, [k0, rk], [('ob', oi)])
                        P.ts('dve', o2[o2i][:], b1[:, col:col + 128], r[:, 2:3], ALU.mult, [k1, rk], [('o2', o2i)])
                        P.tt('pool', ob[oi][:], ob[oi][:], o2[o2i][:], ALU.subtract, [('ob', oi), ('o2', o2i)], [('ob', oi)])
                        row = u * seq + qb * 512 + jq * 128
                        P.dma('sp', oo[row:row + 128, :], ob[oi][:], [('ob', oi)], ())

            n = len(steps)
            emit_S(0)
            emit_S(1)
            for i in range(n):
                if i + 2 < n:
                    emit_S(i + 2)
                emit_AV(i)
        P.finish()
        P.emit()
    return nc


DIL_LIST = (1, 4, 16)
KPAD = 10240


def build_b3(units=tuple(g for g in range(3) for _ in range(4))):
    nc = bass.Bass("TRN2", target_bir_lowering=False)
    NU = len(units)
    qT = _din(nc, "qT", [NU * 64, SEQ], BF16)
    kT = _din(nc, "kT", [NU * 64, KPAD], BF16)
    vp = _din(nc, "vp", [NU * KPAD, 64], BF16)
    oo = _dout(nc, "o", [NU * SEQ, 65])
    with ExitStack() as es:
        C = Ctx(nc, es)
        P = C.P
        M = build_tri_masks(C, None)
        mk = [P.sb("m3_%d" % i, [128, 256], F32) for i in range(3)]
        for i in range(3):
            P.cp('pool', mk[i][:, 0:128], M["ge"][0][:], [M["ge"][1]], [('m3', i)])
            P.cp('pool', mk[i][:, 128:256], M["le"][0][:], [M["le"][1]], [('m3', i)])
        P.memset('pool', mk[1][0:64, 0:128], 0.0, [('m3', 1)])
        P.memset('pool', mk[2][64:128, 128:256], 0.0, [('m3', 2)])
        qs = [P.sb("qs%d" % i, [64, SEQ], BF16) for i in range(2)]
        ks = [P.sb("ks%d" % i, [64, KPAD], BF16) for i in range(2)]
        NVB = 4
        va = [P.sb("va%d" % i, [128, 2, 66], BF16) for i in range(NVB)]
        for i in range(NVB):
            P.memset('pool', va[i][:, :, 64:66], 1.0, [('vones', i)])
        ef = [P.sb("ef%d" % i, [128, 256], F32) for i in range(3)]
        em = [P.sb("em%d" % i, [128, 256], BF16) for i in range(3)]
        osb = [P.sb("osb%d" % i, [128, 65], F32) for i in range(3)]
        scale = 64 ** -0.5
        steps = []
        for u, g in enumerate(units):
            dil = DIL_LIST[g]
            L = SEQ // dil
            for r in range(dil):
                for blk in range(L // 128):
                    mt = 1 if blk == 0 else (2 if blk == L // 128 - 1 else 0)
                    steps.append((u, r * L + blk * 128, r * (L + 128) + blk * 128, mt))
        loaded = set()

        def ensure_unit(u):
            if u in loaded:
                return
            loaded.add(u)
            bi = u % 2
            for cb in range(2):
                P.dma('sp', qs[bi][:, cb * 4096:(cb + 1) * 4096], qT[u * 64:(u + 1) * 64, cb * 4096:(cb + 1) * 4096], (), [('qs', bi, cb)])
                P.dma('sp', ks[bi][:, cb * 5120:(cb + 1) * 5120], kT[u * 64:(u + 1) * 64, cb * 5120:(cb + 1) * 5120], (), [('ks', bi, cb)])

        def kkeys(bi, w0):
            s = {w0 // 5120, (w0 + 255) // 5120}
            return [('ks', bi, x) for x in s]

        def emit_S(i):
            u, q0, w0, mt = steps[i]
            ensure_unit(u)
            bi = u % 2
            j = i % 3
            ps, pk = C.psf[j], ('psf', j)
            rd = kkeys(bi, w0) + [('qs', bi, q0 // 4096)]
            P.mm(ps[:, 0:128], ks[bi][:, w0:w0 + 128], qs[bi][:, q0:q0 + 128], True, False, rd, [pk])
            P.mm(ps[:, 128:256], ks[bi][:, w0 + 128:w0 + 256], qs[bi][:, q0:q0 + 128], False, True, rd, [pk])
            P.act(ef[j][:], ps[:, 0:256], AF.Exp, [pk], [('ef', j)], scale=scale)
            P.tt('dve', em[j][:], ef[j][:], mk[mt][:], ALU.mult, [('ef', j), ('m3', mt)], [('em', j)])
            vi = i % NVB
            P.dma('sp', va[vi][:, :, 0:64], vp[u * KPAD + w0:u * KPAD + w0 + 256, :].rearrange("(t p) d -> p t d", p=128), (), [('va', vi)])

        def emit_AV(i):
            u, q0, w0, mt = steps[i]
            j = i % 3
            vi = i % NVB
            po, pok = C.psf[3 + j], ('psf', 3 + j)
            P.mm(po[:, 0:65], em[j][:, 0:128], va[vi][:, 0, 0:65], True, False, [('em', j), ('va', vi), ('vones', vi)], [pok])
            P.mm(po[:, 0:65], em[j][:, 128:256], va[vi][:, 1, 0:65], False, True, [('em', j), ('va', vi), ('vones', vi)], [pok])
            P.cp('act', osb[j][:], po[:, 0:65], [pok], [('osb', j)])
            P.dma('sp', oo[u * SEQ + q0:u * SEQ + q0 + 128, :], osb[j][:], [('osb', j)], ())

        n = len(steps)
        emit_S(0)
        emit_S(1)
        for i in range(n):
            if i + 2 < n:
                emit_S(i + 2)
            emit_AV(i)
        P.finish()
        P.emit()
    return nc


def build_b2(nunits=2, seq=SEQ):
    nc = bass.Bass("TRN2", target_bir_lowering=False)
    SP = seq + 4
    xbcT = _din(nc, "xbcT", [nunits * 512, SP], BF16)
    cw = _din(nc, "cw", [nunits * 128, 20])
    cb = _din(nc, "cb", [nunits * 128, 4])
    dtin = _din(nc, "dt", [nunits * 128, (seq // 128) * 8])
    alog = _din(nc, "alog", [nunits, 8])
    dsk = _din(nc, "dsk", [nunits, 4])
    yo = {"f": _dout(nc, "yf", [nunits * seq, 256]), "b": _dout(nc, "yb", [nunits * seq, 256])}
    NCH = seq // 128
    TB = 2048
    with ExitStack() as es:
        C = Ctx(nc, es)
        P = C.P
        ident = C.ident
        M = build_tri_masks(C, None)
        NEG = {}
        for d, src in (("f", "le"), ("b", "gt")):
            t = P.sb("neg" + d, [128, 4, 128], F32)
            for hh in range(4):
                P.ts('dve', t[:, hh, :], M[src][0][:], 30000.0, ALU.mult, [M[src][1]], [('neg', d)], s2=-30000.0, op1=ALU.add)
            NEG[d] = t
        xa = P.sb("xa", [128, 4, TB + 4], BF16)
        xb = P.sb("xb", [128, 4, TB + 4], BF16)
        XC = P.sb("XC", [128, 4, seq], BF16)
        Dg = P.sb("Dg", [128, 20, 128], BF16)
        cwt = P.sb("cwt", [128, 20], F32)
        cbt = P.sb("cbt", [128, 4], F32)
        dts = P.sb("dts", [128, NCH, 8], F32)
        aneg = P.sb("aneg", [128, 8], F32)
        dskt = P.sb("dskt", [128, 4], F32)
        cbs = [P.sb("cbs%d" % i, [128, 128], F32) for i in range(2)]
        R = {}
        for d in "fb":
            R[d] = dict(
                xtm=[P.sb("xtm%s%d" % (d, i), [128, 384], BF16) for i in range(2)],
                dta=[P.sb("dta%s%d" % (d, i), [128, 8], F32) for i in range(2)],
                eac=[P.sb("eac%s%d" % (d, i), [128, 16], F32) for i in range(2)],
                nac=[P.sb("nac%s%d" % (d, i), [128, 4], F32) for i in range(2)],
                eal=[P.sb("eal%s%d" % (d, i), [128, 4], F32) for i in range(2)],
                T=[P.sb("T%s%d" % (d, i), [128, 4, 128], F32) for i in range(2)],
                WT=[P.sb("WT%s%d" % (d, i), [128, 4, 128], BF16) for i in range(2)],
                xdt=[P.sb("xdt%s%d" % (d, i), [128, 256], BF16) for i in range(2)],
                xw=[P.sb("xw%s%d" % (d, i), [128, 256], BF16) for i in range(2)],
                yoff=[P.sb("yoff%s%d" % (d, i), [128, 256], F32) for i in range(2)],
                ysb=[P.sb("ysb%s%d" % (d, i), [128, 256], F32) for i in range(2)],
                ST=P.sb("ST" + d, [128, 256], F32),
                STb=[P.sb("STb%s%d" % (d, i), [128, 256], BF16) for i in range(2)],
            )
            for i in range(2):
                P.memset('pool', R[d]["dta"][i][:], 0.0, [('dta', d, i)])
        for u in range(nunits):
            P.dma('sp', cwt[:], cw[u * 128:(u + 1) * 128, :], (), ['cwt'])
            P.dma('sp', cbt[:], cb[u * 128:(u + 1) * 128, :], (), ['cbt'])
            P.dma('sp', dts[:].rearrange("p t e -> p (t e)"), dtin[u * 128:(u + 1) * 128, :], (), ['dts'])
            P.dma('sp', aneg[:], alog[u:u + 1, :].partition_broadcast(128), (), ['aneg'])
            P.dma('sp', dskt[:], dsk[u:u + 1, :].partition_broadcast(128), (), ['dskt'])
            P.act(aneg[:], aneg[:], AF.Exp, ['aneg'], ['aneg'])
            P.ts('dve', aneg[:], aneg[:], -1.0, ALU.mult, ['aneg'], ['aneg'])
            for j in range(20):
                P.ts('dve', Dg[:, j, :], ident[:], cwt[:, j:j + 1], ALU.mult, ['ident', 'cwt'], ['Dg'])
            for blk in range(seq // TB):
                for ct in range(4):
                    r0 = u * 512 + ct * 128
                    P.dma('sp', xa[:, ct, :], xbcT[r0:r0 + 128, blk * TB:blk * TB + TB + 4], (), [('xa', ct)])
                    P.dma('sp', xb[:, ct, 0:TB + 3], xbcT[r0:r0 + 128, blk * TB + 1:blk * TB + TB + 4], (), [('xb', ct)])
                for ct in range(4):
                    for tb in range(TB // 512):
                        ps, pk = C.psum()
                        for w in range(5):
                            src = xa if w % 2 == 0 else xb
                            off = tb * 512 + (w if w % 2 == 0 else w - 1)
                            P.mm(ps[:], Dg[:, ct * 5 + w, :], src[:, ct, off:off + 512], w == 0, w == 4,
                                 ['Dg', ('xa', ct), ('xb', ct)], [pk])
                        g0 = blk * TB + tb * 512
                        P.act(XC[:, ct, g0:g0 + 512], ps[:], AF.Silu, [pk, 'cbt'], [('XC', g0 // 512)], bias=cbt[:, ct:ct + 1])
            for d in "fb":
                P.memset('pool', R[d]["ST"][:], 0.0, [('ST', d)])
                P.memset('pool', R[d]["STb"][0][:], 0.0, [('STb', d, 0)])
                R[d]["si"] = 0
            for idx in range(NCH):
                for d in "fb":
                    n = idx if d == "f" else NCH - 1 - idx
                    Q = R[d]
                    i2 = P.rr('r' + d, 2)
                    t0 = n * 128
                    xk = [('XC', n // 4)]
                    pc, pck = C.psum()
                    P.mm(pc[:, 0:128], XC[:, 2, t0:t0 + 128], XC[:, 3, t0:t0 + 128], True, True, xk, [pck])
                    ci2 = P.rr('cbs', 2)
                    P.cp('act', cbs[ci2][:], pc[:, 0:128], [pck], [('cbs', ci2)])
                    pb, pbk = C.psumb()
                    for ct in range(3):
                        P.tr(pb[:, ct * 128:(ct + 1) * 128], XC[:, ct, t0:t0 + 128], ident[:], xk + ['ident'], [pbk])
                    xtm, kx = Q["xtm"][i2], ('xtm', d, i2)
                    P.cp('act', xtm[:], pb[:, 0:384], [pbk], [kx])
                    dcol = 0 if d == "f" else 4
                    dta, kdta = Q["dta"][i2], ('dta', d, i2)
                    P.tt('dve', dta[:, 0:4], dts[:, n, dcol:dcol + 4], aneg[:, dcol:dcol + 4], ALU.mult, ['dts', 'aneg'], [kdta])
                    minc, minck = M["le"] if d == "f" else M["ge"]
                    mexc, mexck = M["gt"] if d == "f" else M["lt"]
                    pa, pak = C.psum()
                    P.mm(pa[:, 0:8], minc[:], dta[:], True, False, [minck, kdta], [pak])
                    P.mm(pa[:, 8:16], mexc[:], dta[:], False, True, [mexck, kdta], [pak])
                    pr, prk = C.psum()
                    for hh in range(4):
                        P.mm(pr[:, hh * 128:(hh + 1) * 128], dta[:, hh:hh + 1].to_broadcast([128, 128]), minc[:], hh == 0, hh == 3,
                             [kdta, minck], [prk])
                    eac, keac = Q["eac"][i2], ('eac', d, i2)
                    nac, knac = Q["nac"][i2], ('nac', d, i2)
                    eal, keal = Q["eal"][i2], ('eal', d, i2)
                    P.act(eac[:], pa[:, 0:16], AF.Exp, [pak], [keac])
                    P.ts('dve', nac[:], pa[:, 0:4], -1.0, ALU.mult, [pak], [knac])
                    lcol = 127 if d == "f" else 0
                    P.cp('dve', eal[:], pr[:].rearrange("p (h n) -> p h n", n=128)[:, :, lcol], [prk], [keal])
                    P.act(eal[:], eal[:], AF.Exp, [keal], [keal])
                    Tt, kT = Q["T"][i2], ('T', d, i2)
                    P.tt('dve', Tt[:].rearrange("p h n -> p (h n)"), pr[:], NEG[d][:].rearrange("p h n -> p (h n)"), ALU.add,
                         [prk, ('neg', d)], [kT])
                    for hh in range(4):
                        P.act(Tt[:, hh, :], Tt[:, hh, :], AF.Exp, [kT, knac], [kT], bias=nac[:, hh:hh + 1])
                    WT, kWT = Q["WT"][i2], ('WT', d, i2)
                    P.tt('dve', WT[:], Tt[:], cbs[ci2][:].unsqueeze(1).to_broadcast([128, 4, 128]), ALU.mult, [kT, ('cbs', ci2)], [kWT])
                    xdt, kxdt = Q["xdt"][i2], ('xdt', d, i2)
                    xw, kxw = Q["xw"][i2], ('xw', d, i2)
                    P.tt('pool', xdt[:].rearrange("p (h n) -> p h n", n=64), xtm[:, 0:256].rearrange("p (h n) -> p h n", n=64),
                         dts[:, n, dcol:dcol + 4].unsqueeze(2).to_broadcast([128, 4, 64]), ALU.mult, [kx, 'dts'], [kxdt])
                    P.tt('pool', xw[:].rearrange("p (h n) -> p h n", n=64), xdt[:].rearrange("p (h n) -> p h n", n=64),
                         eac[:, 8:12].unsqueeze(2).to_broadcast([128, 4, 64]), ALU.mult, [kxdt, keac], [kxw])
                    pyd, pydk = C.psum()
                    for hh in range(4):
                        P.mm(pyd[:, hh * 64:(hh + 1) * 64], WT[:, hh, :], xdt[:, hh * 64:(hh + 1) * 64], hh == 0, hh == 3, [kWT, kxdt], [pydk])
                    si = Q["si"]
                    pyo, pyok = C.psum()
                    P.mm(pyo[:, 0:256], XC[:, 3, t0:t0 + 128], Q["STb"][si][:], True, True, xk + [('STb', d, si)], [pyok])
                    pst, pstk = C.psum()
                    P.mm(pst[:, 0:256], xtm[:, 256:384], xw[:], True, True, [kx, kxw], [pstk])
                    yoff, kyoff = Q["yoff"][i2], ('yoff', d, i2)
                    P.tt('dve', yoff[:].rearrange("p (h n) -> p h n", n=64), pyo[:, 0:256].rearrange("p (h n) -> p h n", n=64),
                         eac[:, 0:4].unsqueeze(2).to_broadcast([128, 4, 64]), ALU.mult, [pyok, keac], [kyoff])
                    ysb, kysb = Q["ysb"][i2], ('ysb', d, i2)
                    P.tt('dve', ysb[:], pyd[:, 0:256], yoff[:], ALU.add, [pydk, kyoff], [kysb])
                    if d == "f":
                        P.tt('pool', yoff[:].rearrange("p (h n) -> p h n", n=64), xtm[:, 0:256].rearrange("p (h n) -> p h n", n=64),
                             dskt[:].unsqueeze(2).to_broadcast([128, 4, 64]), ALU.mult, [kx, 'dskt', kyoff], [kyoff])
                        P.tt('pool', ysb[:], ysb[:], yoff[:], ALU.add, [kysb, kyoff], [kysb])
                    P.dma('sp', yo[d][u * seq + t0:u * seq + t0 + 128, :], ysb[:], [kysb], ())
                    for hh in range(4):
                        P.stt(Q["ST"][:, hh * 64:(hh + 1) * 64], Q["ST"][:, hh * 64:(hh + 1) * 64], eal[:, hh:hh + 1],
                              pst[:, hh * 64:(hh + 1) * 64], ALU.mult, ALU.add, [('ST', d), keal, pstk], [('ST', d)])
                    nsi = 1 - si
                    P.cp('act', Q["STb"][nsi][:], Q["ST"][:], [('ST', d)], [('STb', d, nsi)])
                    Q["si"] = nsi
                    P.barrier()
        P.finish()
        P.emit()
    return nc


_DEBUG = {}


def _run(nc, ins):
    res = run_bass_kernel_spmd(nc, ins, core_ids=list(range(8)))
    return res.results


def _gather(r, name):
    a = np.concatenate([np.asarray(r[c][name]) for c in range(8)], axis=0)
    return a.reshape(2, SEQ, a.shape[-1])


def stage_a0(W, x):
    ins = []
    for c in range(8):
        b, sl = tok_slice(c)
        ins.append({"h": _c(x[b, sl]), "norm_mix": W["norm_mix"][0:1], "w_in": W["gla_w_in"][0],
                    "wgf": W["gla_w_gate_f"][0], "wgb": W["gla_w_gate_b"][0],
                    "bgf": W["gla_b_gate_f"][0:1], "bgb": W["gla_b_gate_b"][0:1]})
    r = _run(build_ac(None, 0), ins)
    return {n: _gather(r, m) for n, m in (("h", "h_out"), ("q", "q"), ("k", "k"), ("v", "v"), ("sr", "sro"), ("laf", "laf"), ("lab", "lab"))}


def stage_b0(A):
    ins = []
    for c in range(8):
        b, hd = c // 4, c % 4
        hs = slice(hd * 128, (hd + 1) * 128)
        ins.append({"qT": _c(A["q"][b][:, hs].T), "kT": _c(A["k"][b][:, hs].T), "k": _c(A["k"][b][:, hs]),
                    "v": _c(A["v"][b][:, hd * 256:(hd + 1) * 256]), "laf": _c(A["laf"][b][:, hs]), "lab": _c(A["lab"][b][:, hs])})
    r = _run(build_b0(), ins)
    of = np.zeros((2, SEQ, 1024), np.float32)
    ob = np.zeros((2, SEQ, 1024), np.float32)
    for c in range(8):
        b, hd = c // 4, c % 4
        of[b][:, hd * 256:(hd + 1) * 256] = r[c]["of"]
        ob[b][:, hd * 256:(hd + 1) * 256] = r[c]["ob"]
    return of, ob


def stage_ac01(W, A, of, ob, rope):
    ins = []
    for c in range(8):
        b, sl = tok_slice(c)
        d = {"h": _c(A["h"][b, sl]), "of": _c(of[b, sl]), "ob": _c(ob[b, sl]), "sr": _c(A["sr"][b, sl]),
             "g_rep": _c(np.tile(W["gla_g_out"][0], 4)[None, :]), "w_out": W["gla_w_out"][0],
             "norm_mix": W["norm_mix"][1:2], "w_in": W["diff_w_in"][0],
             "gq_rep": _c(np.tile(W["diff_g_q"][0].reshape(-1), 8)[None, :]),
             "gk_rep": _c(np.tile(W["diff_g_k"][0].reshape(-1), 8)[None, :]), "rope": _c(rope[sl])}
        d.update(common_c_inputs(W, 0, c))
        ins.append(d)
    r = _run(build_ac(0, 1), ins)
    return {n: _gather(r, m) for n, m in (("h", "h_out"), ("q", "q"), ("k", "k"), ("v", "v"))}


def stage_b1(W, A):
    lam_init = 0.8 - 0.6 * math.exp(-0.3 * 1)
    lamv = _c(np.concatenate([W["diff_lam_q1"][0], W["diff_lam_k1"][0], W["diff_lam_q2"][0], W["diff_lam_k2"][0]])[None, :])
    ins = []
    for c in range(8):
        b = c // 4
        hds = (2 * (c % 4), 2 * (c % 4) + 1)
        ins.append({"qT": _c(np.concatenate([A["q"][b][:, h * 128:(h + 1) * 128].T for h in hds], axis=0)),
                    "kT": _c(np.concatenate([A["k"][b][:, h * 128:(h + 1) * 128].T for h in hds], axis=0)),
                    "v": _c(np.concatenate([A["v"][b][:, h * 128:(h + 1) * 128] for h in hds], axis=0)),
                    "lamv": lamv})
    r = _run(build_b1(lam_init), ins)
    on = np.zeros((2, SEQ, 1024), np.float32)
    for c in range(8):
        b = c // 4
        for u in range(2):
            h = 2 * (c % 4) + u
            on[b][:, h * 128:(h + 1) * 128] = r[c]["o"][u * SEQ:(u + 1) * SEQ]
    return on


def stage_ac12(W, A, on):
    ins = []
    for c in range(8):
        b, sl = tok_slice(c)
        d = {"h": _c(A["h"][b, sl]), "on": _c(on[b, sl]), "g_rep": _c(np.tile(W["diff_g_sub"][0], 8)[None, :]),
             "w_out": W["diff_w_out"][0], "norm_mix": W["norm_mix"][2:3], "w_in": W["ssd_w_in"][0],
             "dtb": _c(np.concatenate([W["ssd_dt_bias_f"][0], W["ssd_dt_bias_b"][0]])[None, :])}
        d.update(common_c_inputs(W, 1, c))
        ins.append(d)
    r = _run(build_ac(1, 2), ins)
    return {n: _gather(r, m) for n, m in (("h", "h_out"), ("sz", "szo"), ("xbc", "xbc"), ("dt", "dt"))}


def stage_b2(W, A):
    ins = []
    cwf = W["ssd_conv_w"][0]
    cbf = W["ssd_conv_b"][0]
    for c in range(8):
        b = c // 4
        xs, cws, cbs_, cbr, dts, als, dks = [], [], [], [], [], [], []
        for u in range(2):
            g = 2 * (c % 4) + u
            sel = np.r_[g * 256:(g + 1) * 256, 2048 + g * 128:2048 + (g + 1) * 128, 3072 + g * 128:3072 + (g + 1) * 128]
            xt = A["xbc"][b][:, sel].T
            xs.append(np.pad(xt, ((0, 0), (2, 2))))
            cws.append(cwf[:, sel].T.reshape(4, 128, 5).transpose(1, 0, 2).reshape(128, 20))
            cbs_.append(cbf[sel].reshape(4, 128).T)
            cbr.append(cbf[sel][:384])
            dtu = np.concatenate([A["dt"][b][:, 4 * g:4 * g + 4], A["dt"][b][:, 32 + 4 * g:32 + 4 * g + 4]], axis=1)
            dts.append(dtu.reshape(SEQ // 128, 128, 8).transpose(1, 0, 2).reshape(128, (SEQ // 128) * 8))
            als.append(np.concatenate([W["ssd_a_log_f"][0][4 * g:4 * g + 4], W["ssd_a_log_b"][0][4 * g:4 * g + 4]]))
            dks.append(W["ssd_d"][0][4 * g:4 * g + 4])
        ins.append({"xbcT": _c(np.concatenate(xs, axis=0)), "cw": _c(np.concatenate(cws, axis=0)), "cb": _c(np.concatenate(cbs_, axis=0)),
                    "dt": _c(np.concatenate(dts, axis=0)), "alog": _c(np.stack(als)), "dsk": _c(np.stack(dks))})
    r = _run(build_b2(), ins)
    yf = np.zeros((2, SEQ, 2048), np.float32)
    yb = np.zeros((2, SEQ, 2048), np.float32)
    for c in range(8):
        b = c // 4
        for u in range(2):
            g = 2 * (c % 4) + u
            yf[b][:, g * 256:(g + 1) * 256] = r[c]["yf"][u * SEQ:(u + 1) * SEQ]
            yb[b][:, g * 256:(g + 1) * 256] = r[c]["yb"][u * SEQ:(u + 1) * SEQ]
    return yf, yb


def stage_ac23(W, A, yf, yb, rope):
    gq = _c(np.concatenate([np.tile(W["dil_g_q"][0][g], 16) for g in range(3)])[None, :])
    gk = _c(np.concatenate([np.tile(W["dil_g_k"][0][g], 16) for g in range(3)])[None, :])
    ins = []
    for c in range(8):
        b, sl = tok_slice(c)
        d = {"h": _c(A["h"][b, sl]), "yf": _c(yf[b, sl]), "yb": _c(yb[b, sl]), "sz": _c(A["sz"][b, sl]),
             "g_rep": W["ssd_g_norm"][0:1], "w_out": W["ssd_w_out"][0], "norm_mix": W["norm_mix"][3:4],
             "w_in": W["dil_w_in"][0], "gq_rep": gq, "gk_rep": gk, "rope": _c(rope[sl])}
        d.update(common_c_inputs(W, 2, c))
        ins.append(d)
    r = _run(build_ac(2, 3), ins)
    return {n: _gather(r, m) for n, m in (("h", "h_out"), ("q", "q"), ("k", "k"), ("v", "v"))}


def stage_b3(A):
    ins = []
    meta = []
    for c in range(8):
        b = c // 4
        qs, ks, vs = [], [], []
        um = []
        for g in range(3):
            dil = DIL_LIST[g]
            L = SEQ // dil
            for j in range(4):
                hd = 4 * (c % 4) + j
                cs = slice(g * 1024 + hd * 64, g * 1024 + (hd + 1) * 64)
                q = A["q"][b][:, cs]; k = A["k"][b][:, cs]; v = A["v"][b][:, cs]
                qs.append(np.concatenate([q[r::dil] for r in range(dil)], axis=0).T)
                kp = np.concatenate([np.pad(k[r::dil], ((64, 64), (0, 0))) for r in range(dil)], axis=0)
                vpd = np.concatenate([np.pad(v[r::dil], ((64, 64), (0, 0))) for r in range(dil)], axis=0)
                ks.append(np.pad(kp, ((0, KPAD - kp.shape[0]), (0, 0))).T)
                vs.append(np.pad(vpd, ((0, KPAD - vpd.shape[0]), (0, 0))))
                um.append((g, hd, dil, L))
        meta.append(um)
        ins.append({"qT": _c(np.concatenate(qs, axis=0)), "kT": _c(np.concatenate(ks, axis=0)), "vp": _c(np.concatenate(vs, axis=0))})
    r = _run(build_b3(), ins)
    oa = np.zeros((2, SEQ, 3, 16, 65), np.float32)
    for c in range(8):
        b = c // 4
        o = np.asarray(r[c]["o"]).reshape(12, SEQ, 65)
        for u, (g, hd, dil, L) in enumerate(meta[c]):
            for rr in range(dil):
                oa[b, rr::dil, g, hd, :] = o[u, rr * L:(rr + 1) * L]
    return oa.reshape(2, SEQ, 3 * 16 * 65)


def stage_c3(W, A, oa):
    ins = []
    for c in range(8):
        b, sl = tok_slice(c)
        d = {"h": _c(A["h"][b, sl]), "oa": _c(oa[b, sl]), "w_out": W["dil_w_out"][0]}
        d.update(common_c_inputs(W, 3, c))
        ins.append(d)
    r = _run(build_ac(3, None), ins)
    return _gather(r, "out")


def kernel(**inp):
    W = {k: np.asarray(v) for k, v in inp.items()}
    x = W["x"]
    rope = rope_table()
    A0 = stage_a0(W, x)
    of, ob = stage_b0(A0)
    A1 = stage_ac01(W, A0, of, ob, rope)
    _DEBUG["h0"] = A1["h"]
    on = stage_b1(W, A1)
    A2 = stage_ac12(W, A1, on)
    _DEBUG["h1"] = A2["h"]
    yf, yb = stage_b2(W, A2)
    A3 = stage_ac23(W, A2, yf, yb, rope)
    _DEBUG["h2"] = A3["h"]
    oa = stage_b3(A3)
    out = stage_c3(W, A3, oa)
    out = np.ascontiguousarray(out.astype(np.float32))
    if os.environ.get("KDBG_DIR"):
        for kk, vv in _DEBUG.items():
            np.save(os.path.join(os.environ["KDBG_DIR"], "kdbg_%s.npy" % kk), vv)
    return out
```

```python
import math
import os
import numpy as np
import ml_dtypes
import concourse.bass as bass
import concourse.mybir as mybir
from concourse.bass_utils import run_bass_kernel_spmd
from contextlib import ExitStack

F32 = mybir.dt.float32
BF16 = mybir.dt.bfloat16
AF = mybir.ActivationFunctionType
ALU = mybir.AluOpType
AX = mybir.AxisListType
NPBF = ml_dtypes.bfloat16

NDS = 8
EPS = 1e-6
TOK = 2048
NT = 16
D = 1024
DFF = 2816
SEQ = 8192


class Prog:
    def __init__(self, nc, es):
        self.nc, self.es = nc, es
        self.q = {e: [] for e in ('pe', 'act', 'dve', 'pool', 'sp')}
        self.sem = {e: es.enter_context(nc.semaphore("s_" + e)) for e in ('pe', 'act', 'dve', 'pool')}
        self.cnt = {e: 0 for e in self.sem}
        self.dsem = {qn: [es.enter_context(nc.semaphore("d_%s%d" % (qn, i))) for i in range(NDS)]
                     for qn in ('sp', 'pool')}
        self.dcnt = {qn: [0] * NDS for qn in self.dsem}
        self.drr = {qn: 0 for qn in self.dsem}
        self.waited = {e: {} for e in self.q}
        self.lastw = {}
        self.readers = {}
        self.nuniq = 0
        self.rot = {}

    def sb(self, name, shape, dt):
        return self.es.enter_context(self.nc.sbuf_tensor("sb_" + name, list(shape), dt))

    def ps(self, name, shape, dt=F32):
        return self.es.enter_context(self.nc.psum_tensor("pp_" + name, list(shape), dt))

    def rr(self, name, n):
        i = self.rot.get(name, 0)
        self.rot[name] = (i + 1) % n
        return i

    def _collect(self, eng, reads, writes):
        deps = []
        for k in reads:
            t = self.lastw.get(k)
            if t is not None:
                deps.append(t)
        for k in writes:
            t = self.lastw.get(k)
            if t is not None:
                deps.append(t)
            deps.extend(self.readers.get(k, {}).values())
        waits = []
        w = self.waited[eng]
        for (sem, val, e2) in deps:
            if e2 == eng and eng == 'pe':
                continue
            if w.get(id(sem), 0) < val:
                w[id(sem)] = val
                waits.append((sem, val))
        return waits

    def _commit(self, tok, rkey, reads, writes):
        for k in writes:
            self.lastw[k] = tok
            self.readers[k] = {}
        for k in reads:
            self.readers.setdefault(k, {})[rkey] = tok

    def op(self, eng, fn, reads=(), writes=()):
        waits = self._collect(eng, reads, writes)
        self.cnt[eng] += 1
        sem = self.sem[eng]
        tok = (sem, self.cnt[eng], eng)
        self.q[eng].append((waits, fn, sem, 1))
        self._commit(tok, eng, reads, writes)

    def dma(self, qn, out, in_, reads=(), writes=()):
        waits = self._collect(qn, reads, writes)
        j = self.drr[qn]
        self.drr[qn] = (j + 1) % NDS
        sem = self.dsem[qn][j]
        prev = 16 * self.dcnt[qn][j]
        w = self.waited[qn]
        if prev > 0 and w.get(id(sem), 0) < prev:
            w[id(sem)] = prev
            waits.append((sem, prev))
        self.dcnt[qn][j] += 1
        tok = (sem, 16 * self.dcnt[qn][j], 'dma')
        self.q[qn].append((waits, (lambda e: e.dma_start(out=out, in_=in_)), sem, 16))
        self.nuniq += 1
        self._commit(tok, ('dma', self.nuniq), reads, writes)

    def barrier(self):
        targets = [(self.sem[e], self.cnt[e]) for e in self.sem if self.cnt[e] > 0]
        for qn in self.dsem:
            for j in range(NDS):
                if self.dcnt[qn][j] > 0:
                    targets.append((self.dsem[qn][j], 16 * self.dcnt[qn][j]))
        for eng in ('pe', 'act', 'dve', 'pool', 'sp'):
            w = self.waited[eng]
            waits = []
            for (sem, val) in targets:
                if sem is self.sem.get(eng):
                    continue
                if w.get(id(sem), 0) < val:
                    w[id(sem)] = val
                    waits.append((sem, val))
            if waits:
                self.q[eng].append((waits, None, None, 0))

    def finish(self):
        waits = []
        for qn in self.dsem:
            for j in range(NDS):
                if self.dcnt[qn][j] > 0:
                    waits.append((self.dsem[qn][j], 16 * self.dcnt[qn][j]))
        for e in self.sem:
            if self.cnt[e] > 0:
                waits.append((self.sem[e], self.cnt[e]))
        self.q['sp'].append((waits, None, None, 0))

    def emit(self):
        nc = self.nc
        with nc.Block() as block:
            def mk(name):
                def body(eng):
                    for (waits, fn, sem, inc) in self.q[name]:
                        for (s, v) in waits:
                            eng.wait_ge(s, v)
                        if fn is not None:
                            fn(eng).then_inc(sem, inc)
                return body
            block.tensor(mk('pe'))
            block.scalar(mk('act'))
            block.vector(mk('dve'))
            block.gpsimd(mk('pool'))
            block.sync(mk('sp'))

    def mm(self, out, lhsT, rhs, start, stop, reads, writes):
        self.op('pe', lambda e: e.matmul(out, lhsT=lhsT, rhs=rhs, start=start, stop=stop), reads, writes)

    def tr(self, out, in_, ident, reads, writes):
        self.op('pe', lambda e: e.transpose(out=out, in_=in_, identity=ident), reads, writes)

    def act(self, out, in_, func, reads, writes, **kw):
        self.op('act', lambda e: e.activation(out=out, in_=in_, func=func, **kw), reads, writes)

    def tt(self, eng, out, in0, in1, op, reads, writes):
        self.op(eng, lambda e: e.tensor_tensor(out=out, in0=in0, in1=in1, op=op), reads, writes)

    def ts(self, eng, out, in0, s1, op0, reads, writes, s2=None, op1=None):
        if op1 is None:
            self.op(eng, lambda e: e.tensor_scalar(out=out, in0=in0, scalar1=s1, scalar2=None, op0=op0), reads, writes)
        else:
            self.op(eng, lambda e: e.tensor_scalar(out=out, in0=in0, scalar1=s1, scalar2=s2, op0=op0, op1=op1), reads, writes)

    def stt(self, out, in0, scalar, in1, op0, op1, reads, writes):
        self.op('dve', lambda e: e.scalar_tensor_tensor(out=out, in0=in0, scalar=scalar, in1=in1, op0=op0, op1=op1), reads, writes)

    def cp(self, eng, out, in_, reads, writes):
        if eng == 'act':
            self.op('act', lambda e: e.activation(out=out, in_=in_, func=AF.Copy), reads, writes)
        else:
            self.op(eng, lambda e: e.tensor_copy(out=out, in_=in_), reads, writes)

    def red(self, out, in_, reads, writes, op=ALU.add):
        self.op('dve', lambda e: e.tensor_reduce(out=out, in_=in_, axis=AX.X, op=op), reads, writes)

    def recip(self, out, in_, reads, writes):
        self.op('dve', lambda e: e.reciprocal(out=out, in_=in_), reads, writes)

    def memset(self, eng, ap, val, writes):
        self.op(eng, lambda e: e.memset(ap, val), (), writes)

    def asel(self, out, in_, pattern, cmp, fill, base, cm, reads, writes):
        self.op('pool', lambda e: e.affine_select(out=out, in_=in_, pattern=pattern, compare_op=cmp, fill=fill,
                                                  base=base, channel_multiplier=cm), reads, writes)


class Ctx:
    def __init__(self, nc, es):
        self.nc = nc
        self.P = P = Prog(nc, es)
        self.psf = [P.ps("psf%d" % i, [128, 512], F32) for i in range(6)]
        self.psb = [P.ps("psb%d" % i, [128, 1024], BF16) for i in range(2)]
        self.ident = P.sb("ident", [128, 128], BF16)
        P.memset('pool', self.ident[:], 1.0, ['ident'])
        P.asel(self.ident[:], self.ident[:], [[-1, 128]], ALU.is_equal, 0.0, 0, 1, ['ident'], ['ident'])

    def psum(self):
        i = self.P.rr('psf', 6)
        return self.psf[i], ('psf', i)

    def psumb(self):
        i = self.P.rr('psb', 2)
        return self.psb[i], ('psb', i)


def _din(nc, name, shape, dt=F32):
    return nc.dram_tensor(name, list(shape), dt, kind="ExternalInput").ap()


def _dout(nc, name, shape, dt=F32):
    return nc.dram_tensor(name, list(shape), dt, kind="ExternalOutput").ap()


def build_ac(lc, la):
    nc = bass.Bass("TRN2", target_bir_lowering=False)
    h_in = _din(nc, "h", [TOK, D])
    I = {}
    O = {}
    if lc is not None:
        for nm, shp in (("norm_ffn", [1, D]), ("ffn_w_in", [D, 2 * DFF]), ("ffn_w_out", [DFF, D]),
                        ("ple_norm", [1, D]), ("ple_w_gate", [D, D]), ("ple_w_proj", [256, D]), ("pT", [256, TOK])):
            I[nm] = _din(nc, nm, shp)
        if lc == 0:
            I["of"] = _din(nc, "of", [TOK, D]); I["ob"] = _din(nc, "ob", [TOK, D])
            I["sr"] = _din(nc, "sr", [TOK, D], BF16); I["g_rep"] = _din(nc, "g_rep", [1, D])
            I["w_out"] = _din(nc, "w_out", [D, D])
        elif lc == 1:
            I["on"] = _din(nc, "on", [TOK, D]); I["g_rep"] = _din(nc, "g_rep", [1, D])
            I["w_out"] = _din(nc, "w_out", [D, D])
        elif lc == 2:
            I["yf"] = _din(nc, "yf", [TOK, 2048]); I["yb"] = _din(nc, "yb", [TOK, 2048])
            I["sz"] = _din(nc, "sz", [TOK, 2048], BF16); I["g_rep"] = _din(nc, "g_rep", [1, 2048])
            I["w_out"] = _din(nc, "w_out", [2048, D])
        elif lc == 3:
            I["oa"] = _din(nc, "oa", [TOK, 3 * 16 * 65]); I["w_out"] = _din(nc, "w_out", [D, D])
    if la is not None:
        I["norm_mix"] = _din(nc, "norm_mix", [1, D])
        O["h_out"] = _dout(nc, "h_out", [TOK, D])
        if la == 0:
            I["w_in"] = _din(nc, "w_in", [D, 3104])
            I["wgf"] = _din(nc, "wgf", [16, 512]); I["wgb"] = _din(nc, "wgb", [16, 512])
            I["bgf"] = _din(nc, "bgf", [1, 512]); I["bgb"] = _din(nc, "bgb", [1, 512])
            O["q"] = _dout(nc, "q", [TOK, 512]); O["k"] = _dout(nc, "k", [TOK, 512])
            O["v"] = _dout(nc, "v", [TOK, 1024], BF16); O["sr"] = _dout(nc, "sro", [TOK, 1024], BF16)
            O["laf"] = _dout(nc, "laf", [TOK, 512]); O["lab"] = _dout(nc, "lab", [TOK, 512])
        elif la == 1:
            I["w_in"] = _din(nc, "w_in", [D, 3072])
            I["gq_rep"] = _din(nc, "gq_rep", [1, 1024]); I["gk_rep"] = _din(nc, "gk_rep", [1, 1024])
            I["rope"] = _din(nc, "rope", [TOK, 128])
            O["q"] = _dout(nc, "q", [TOK, 1024], BF16); O["k"] = _dout(nc, "k", [TOK, 1024], BF16)
            O["v"] = _dout(nc, "v", [TOK, 1024], BF16)
        elif la == 2:
            I["w_in"] = _din(nc, "w_in", [D, 6208]); I["dtb"] = _din(nc, "dtb", [1, 64])
            O["sz"] = _dout(nc, "szo", [TOK, 2048], BF16); O["xbc"] = _dout(nc, "xbc", [TOK, 4096], BF16)
            O["dt"] = _dout(nc, "dt", [TOK, 64])
        elif la == 3:
            I["w_in"] = _din(nc, "w_in", [D, 9216])
            I["gq_rep"] = _din(nc, "gq_rep", [1, 3072]); I["gk_rep"] = _din(nc, "gk_rep", [1, 3072])
            I["rope"] = _din(nc, "rope", [TOK, 128])
            O["q"] = _dout(nc, "q", [TOK, 3072], BF16); O["k"] = _dout(nc, "k", [TOK, 3072], BF16)
            O["v"] = _dout(nc, "v", [TOK, 3072], BF16)
    else:
        O["out"] = _dout(nc, "out", [TOK, D])

    with ExitStack() as es:
        C = Ctx(nc, es)
        P = C.P
        h = P.sb("h", [128, NT, D], F32)
        xT = P.sb("xT", [128, 8, TOK], BF16)
        actT = P.sb("actT", [128, 4, TOK], BF16)
        NW = 4
        wsl = [P.sb("w%d" % i, [128, 8, 512], BF16) for i in range(NW)]
        gt = [P.sb("gt%d" % i, [128, D], F32) for i in range(2)]
        scr = [P.sb("scr%d" % i, [128, D], F32) for i in range(3)]
        hnb = [P.sb("hnb%d" % i, [128, D], BF16) for i in range(2)]
        stg = [P.sb("stg%d" % i, [128, 512], F32) for i in range(4)]
        sml = [P.sb("sml%d" % i, [128, 64], F32) for i in range(4)]
        ptl = [P.sb("ptl%d" % i, [128, 2, 128], BF16) for i in range(2)]
        ident = C.ident

        def scratch():
            i = P.rr('scr', 3)
            return scr[i], ('scr', i)

        def staging():
            i = P.rr('stg', 4)
            return stg[i], ('stg', i)

        def small():
            i = P.rr('sml', 4)
            return sml[i], ('sml', i)

        def wslot():
            i = P.rr('w', NW)
            return wsl[i], ('w', i)

        hv = h_in.rearrange("(t p) d -> p t d", p=128)
        for t4 in range(4):
            P.dma('sp', h[:, 4 * t4:4 * t4 + 4, :], hv[:, 4 * t4:4 * t4 + 4, :], (), [('h', t) for t in range(4 * t4, 4 * t4 + 4)])

        def load_gain(slot, ap, n=D, off=0):
            P.dma('sp', gt[slot][:, 0:n], ap[0:1, off:off + n].partition_broadcast(128), (), [('gt', slot)])

        def load_w(W, k0, kc, n0, nw):
            wt, wk = wslot()
            src = W[k0 * 128:(k0 + kc) * 128, n0:n0 + nw].rearrange("(c p) n -> p c n", p=128)
            P.dma('pool', wt[:, 0:kc, 0:nw], src, (), [wk])
            return wt, wk

        def rstd_from_ss(ss_ap, ss_key, n_over, cols):
            P.act(ss_ap, ss_ap, AF.Sqrt, [ss_key], [ss_key], scale=1.0 / n_over, bias=EPS)
            P.recip(ss_ap, ss_ap, [ss_key], [ss_key])

        def transpose_to_xT(src_bf, src_key, t, nchunk=8):
            pb, pbk = C.psumb()
            for c in range(nchunk):
                P.tr(pb[:, c * 128:(c + 1) * 128], src_bf[:, c * 128:(c + 1) * 128], ident[:], [src_key, 'ident'], [pbk])
            eng = 'act' if (t % 2 == 0) else 'dve'
            P.cp(eng, xT[:, 0:nchunk, t * 128:(t + 1) * 128], pb[:, 0:nchunk * 128].rearrange("p (c n) -> p c n", n=128),
                 [pbk], [('xT', t)])

        def norm_T(gslot):
            for t in range(NT):
                sq, sqk = scratch()
                sm, smk = small()
                P.act(sq[:], h[:, t, :], AF.Square, [('h', t)], [sqk, smk], accum_out=sm[:, 0:1])
                rstd_from_ss(sm[:, 0:1], smk, D, 1)
                i = P.rr('hnb', 2)
                P.stt(hnb[i][:], h[:, t, :], sm[:, 0:1], gt[gslot][:], ALU.mult, ALU.mult,
                      [('h', t), smk, ('gt', gslot)], [('hnb', i)])
                transpose_to_xT(hnb[i], ('hnb', i), t)

        def linear_tm(W, kc, n0, n1, epilogue, k0=0, xsrc=None, xkeys=None):
            xs = xT if xsrc is None else xsrc
            for nb in range(n0, n1, 512):
                nw = min(512, n1 - nb)
                wt, wk = load_w(W, k0, kc, nb, nw)
                for t in range(NT):
                    ps, pk = C.psum()
                    for c in range(kc):
                        P.mm(ps[:, 0:nw], xs[:, c, t * 128:(t + 1) * 128], wt[:, c, 0:nw], c == 0, c == kc - 1,
                             [('xT', t), wk], [pk])
                    epilogue(t, nb, nw, ps, pk)

        def add_to_h(t, nb, nw, ps, pk):
            P.tt('dve', h[:, t, nb:nb + nw], h[:, t, nb:nb + nw], ps[:, 0:nw], ALU.add, [('h', t), pk], [('h', t)])

        def store_rows(dst, t, c0, cw, src_ap, src_key):
            P.dma('sp', dst[t * 128:(t + 1) * 128, c0:c0 + cw], src_ap, [src_key], ())

        if lc is not None:
            if lc == 0:
                load_gain(0, I["g_rep"])
                for t in range(NT):
                    a, ak = scratch(); b, bk = scratch()
                    P.dma('sp', a[:], I["of"][t * 128:(t + 1) * 128, :], (), [ak])
                    P.dma('sp', b[:], I["ob"][t * 128:(t + 1) * 128, :], (), [bk])
                    i = P.rr('hnb', 2)
                    P.dma('sp', hnb[i][:], I["sr"][t * 128:(t + 1) * 128, :], (), [('hnb', i)])
                    P.tt('dve', a[:], a[:], b[:], ALU.add, [ak, bk], [ak])
                    sm, smk = small()
                    P.act(b[:], a[:], AF.Square, [ak], [bk])
                    P.red(sm[:, 0:4], b[:].rearrange("p (g n) -> p g n", n=256), [bk], [smk])
                    rstd_from_ss(sm[:, 0:4], smk, 256, 4)
                    P.tt('dve', a[:].rearrange("p (g n) -> p g n", n=256), a[:].rearrange("p (g n) -> p g n", n=256),
                         sm[:, 0:4].unsqueeze(2).to_broadcast([128, 4, 256]), ALU.mult, [ak, smk], [ak])
                    P.tt('pool', a[:], a[:], gt[0][:], ALU.mult, [ak, ('gt', 0)], [ak])
                    P.tt('dve', hnb[i][:], a[:], hnb[i][:], ALU.mult, [ak, ('hnb', i)], [('hnb', i)])
                    transpose_to_xT(hnb[i], ('hnb', i), t)
                linear_tm(I["w_out"], 8, 0, D, add_to_h)
            elif lc == 1:
                load_gain(0, I["g_rep"])
                lam_init = 0.8 - 0.6 * math.exp(-0.3 * 1)
                for t in range(NT):
                    a, ak = scratch(); b, bk = scratch()
                    P.dma('sp', a[:], I["on"][t * 128:(t + 1) * 128, :], (), [ak])
                    sm, smk = small()
                    P.act(b[:], a[:], AF.Square, [ak], [bk])
                    P.red(sm[:, 0:8], b[:].rearrange("p (g n) -> p g n", n=128), [bk], [smk])
                    rstd_from_ss(sm[:, 0:8], smk, 128, 8)
                    P.tt('dve', a[:].rearrange("p (g n) -> p g n", n=128), a[:].rearrange("p (g n) -> p g n", n=128),
                         sm[:, 0:8].unsqueeze(2).to_broadcast([128, 8, 128]), ALU.mult, [ak, smk], [ak])
                    i = P.rr('hnb', 2)
                    P.stt(hnb[i][:], a[:], 1.0 - lam_init, gt[0][:], ALU.mult, ALU.mult, [ak, ('gt', 0)], [('hnb', i)])
                    transpose_to_xT(hnb[i], ('hnb', i), t)
                linear_tm(I["w_out"], 8, 0, D, add_to_h)
            elif lc == 2:
                for half in range(2):
                    load_gain(0, I["g_rep"], D, half * D)
                    for t in range(NT):
                        a, ak = scratch(); b, bk = scratch()
                        P.dma('sp', a[:], I["yf"][t * 128:(t + 1) * 128, half * D:(half + 1) * D], (), [ak])
                        P.dma('sp', b[:], I["yb"][t * 128:(t + 1) * 128, half * D:(half + 1) * D], (), [bk])
                        i = P.rr('hnb', 2)
                        P.dma('sp', hnb[i][:], I["sz"][t * 128:(t + 1) * 128, half * D:(half + 1) * D], (), [('hnb', i)])
                        P.tt('dve', a[:], a[:], b[:], ALU.add, [ak, bk], [ak])
                        P.tt('dve', a[:], a[:], hnb[i][:], ALU.mult, [ak, ('hnb', i)], [ak])
                        sm, smk = small()
                        P.act(b[:], a[:], AF.Square, [ak], [bk])
                        P.red(sm[:, 0:4], b[:].rearrange("p (g n) -> p g n", n=256), [bk], [smk])
                        rstd_from_ss(sm[:, 0:4], smk, 256, 4)
                        P.tt('dve', a[:].rearrange("p (g n) -> p g n", n=256), a[:].rearrange("p (g n) -> p g n", n=256),
                             sm[:, 0:4].unsqueeze(2).to_broadcast([128, 4, 256]), ALU.mult, [ak, smk], [ak])
                        P.tt('dve', hnb[i][:], a[:], gt[0][:], ALU.mult, [ak, ('gt', 0), ('hnb', i)], [('hnb', i)])
                        transpose_to_xT(hnb[i], ('hnb', i), t)
                    linear_tm(I["w_out"], 8, 0, D, add_to_h, k0=8 * half)
            elif lc == 3:
                oav = I["oa"].rearrange("n (g x) -> n g x", g=3)
                oat = P.sb("oat", [128, 3, 1040], F32)
                for t in range(NT):
                    P.dma('sp', oat[:], oav[t * 128:(t + 1) * 128, :, :], (), ['oat'])
                    P.tt('dve', oat[:, 0, :], oat[:, 0, :], oat[:, 1, :], ALU.add, ['oat'], ['oat'])
                    P.tt('dve', oat[:, 0, :], oat[:, 0, :], oat[:, 2, :], ALU.add, ['oat'], ['oat'])
                    sv = oat[:, 0, :].rearrange("p (h x) -> p h x", x=65)
                    sm, smk = small()
                    P.recip(sm[:, 0:16], sv[:, :, 64], ['oat'], [smk])
                    i = P.rr('hnb', 2)
                    P.tt('dve', hnb[i][:].rearrange("p (h x) -> p h x", x=64), sv[:, :, 0:64],
                         sm[:, 0:16].unsqueeze(2).to_broadcast([128, 16, 64]), ALU.mult, ['oat', smk], [('hnb', i)])
                    transpose_to_xT(hnb[i], ('hnb', i), t)
                linear_tm(I["w_out"], 8, 0, D, add_to_h)
            load_gain(1, I["norm_ffn"])
            norm_T(1)
            NCH = DFF // 128
            for g0 in range(0, NCH, 4):
                gc = min(4, NCH - g0)
                fw = gc * 128
                wg, wgk = load_w(I["ffn_w_in"], 0, 8, g0 * 128, fw)
                wu, wuk = load_w(I["ffn_w_in"], 0, 8, DFF + g0 * 128, fw)
                wo, wok = wslot()
                wov = wo[:].rearrange("p a n -> p (a n)")
                P.dma('pool', wov[:, 0:gc * 1024].rearrange("p (c n) -> p c n", n=1024),
                      I["ffn_w_out"][g0 * 128:(g0 + gc) * 128, :].rearrange("(c p) n -> p c n", p=128), (), [wok])
                for c in range(gc):
                    for tb in range(4):
                        pg, pgk = C.psum(); pu, puk = C.psum()
                        xk = [('xT', t) for t in range(4 * tb, 4 * tb + 4)]
                        for k in range(8):
                            P.mm(pg[:], wg[:, k, c * 128:(c + 1) * 128], xT[:, k, tb * 512:(tb + 1) * 512], k == 0, k == 7, xk + [wgk], [pgk])
                        for k in range(8):
                            P.mm(pu[:], wu[:, k, c * 128:(c + 1) * 128], xT[:, k, tb * 512:(tb + 1) * 512], k == 0, k == 7, xk + [wuk], [puk])
                        sg, sgk = staging()
                        P.act(sg[:], pg[:], AF.Silu, [pgk], [sgk])
                        P.tt('dve', actT[:, c, tb * 512:(tb + 1) * 512], sg[:], pu[:], ALU.mult, [sgk, puk], [('actT', c, tb)])
                for t in range(NT):
                    for nh in range(2):
                        py, pyk = C.psum()
                        for c in range(gc):
                            P.mm(py[:], actT[:, c, t * 128:(t + 1) * 128], wov[:, c * 1024 + nh * 512:c * 1024 + nh * 512 + 512],
                                 c == 0, c == gc - 1, [('actT', c, t // 4), wok], [pyk])
                        add_to_h(t, nh * 512, 512, py, pyk)
            load_gain(0, I["ple_norm"])
            norm_T(0)
            pTv = I["pT"]
            for nh in range(2):
                wgt, wgtk = load_w(I["ple_w_gate"], 0, 8, nh * 512, 512)
                wpj, wpjk = load_w(I["ple_w_proj"], 0, 2, nh * 512, 512)
                for t in range(NT):
                    i = P.rr('ptl', 2)
                    P.dma('pool', ptl[i][:], pTv[:, t * 128:(t + 1) * 128].rearrange("(c p) n -> p c n", p=128), (), [('ptl', i)])
                    pga, pgak = C.psum(); ppr, pprk = C.psum()
                    for k in range(8):
                        P.mm(pga[:], xT[:, k, t * 128:(t + 1) * 128], wgt[:, k, :], k == 0, k == 7, [('xT', t), wgtk], [pgak])
                    for k in range(2):
                        P.mm(ppr[:], ptl[i][:, k, :], wpj[:, k, :], k == 0, k == 1, [('ptl', i), wpjk], [pprk])
                    sg, sgk = staging()
                    P.act(sg[:], pga[:], AF.Sigmoid, [pgak], [sgk])
                    P.tt('dve', sg[:], sg[:], ppr[:], ALU.mult, [sgk, pprk], [sgk])
                    P.tt('pool', h[:, t, nh * 512:(nh + 1) * 512], h[:, t, nh * 512:(nh + 1) * 512], sg[:], ALU.add,
                         [('h', t), sgk], [('h', t)])

        if la is None:
            ov = O["out"].rearrange("(t p) d -> p t d", p=128)
            for t4 in range(4):
                P.dma('sp', ov[:, 4 * t4:4 * t4 + 4, :], h[:, 4 * t4:4 * t4 + 4, :], [('h', t) for t in range(4 * t4, 4 * t4 + 4)], ())
        else:
            ov = O["h_out"].rearrange("(t p) d -> p t d", p=128)
            for t4 in range(4):
                P.dma('sp', ov[:, 4 * t4:4 * t4 + 4, :], h[:, 4 * t4:4 * t4 + 4, :], [('h', t) for t in range(4 * t4, 4 * t4 + 4)], ())
            load_gain(1, I["norm_mix"])
            norm_T(1)
            W = I["w_in"]

            def ep_copy(dst, c_off, bf, func=AF.Copy, scale=None):
                def ep(t, nb, nw, ps, pk):
                    s, sk = staging()
                    sv = s[:].bitcast(BF16)[:, 0:nw] if bf else s[:, 0:nw]
                    kw = {} if scale is None else {"scale": scale}
                    P.act(sv, ps[:, 0:nw], func, [pk], [sk], **kw)
                    store_rows(dst, t, nb - c_off, nw, sv, sk)
                return ep

            def make_ep_normrope(dst, c_off, gslot, hd=64):
                def ep(t, nb, nw, ps, pk):
                    ns = nw // hd
                    a, ak = scratch(); b, bk = scratch()
                    sm, smk = small()
                    P.act(a[:, 0:nw], ps[:, 0:nw], AF.Square, [pk], [ak])
                    P.red(sm[:, 0:ns], a[:, 0:nw].rearrange("p (g n) -> p g n", n=hd), [ak], [smk])
                    rstd_from_ss(sm[:, 0:ns], smk, hd, ns)
                    P.tt('dve', a[:, 0:nw].rearrange("p (g n) -> p g n", n=hd), ps[:, 0:nw].rearrange("p (g n) -> p g n", n=hd),
                         sm[:, 0:ns].unsqueeze(2).to_broadcast([128, ns, hd]), ALU.mult, [pk, smk], [ak])
                    goff = nb - c_off
                    P.tt('pool', a[:, 0:nw], a[:, 0:nw], gt[gslot][:, goff % D:goff % D + nw], ALU.mult, [ak, ('gt', gslot)], [ak])
                    rp = ropet[t % 2]
                    rk = ('rope', t % 2)
                    av = a[:, 0:nw].rearrange("p (g two n) -> p g two n", two=2, n=hd // 2)
                    bv = b[:, 0:nw].rearrange("p (g two n) -> p g two n", two=2, n=hd // 2)
                    sinv = rp[:, hd:2 * hd].rearrange("p (two n) -> p two n", two=2)
                    P.tt('pool', bv[:, :, 0, :], av[:, :, 1, :], sinv[:, 0:1, :].to_broadcast([128, ns, hd // 2]), ALU.mult, [ak, rk], [bk])
                    P.tt('pool', bv[:, :, 1, :], av[:, :, 0, :], sinv[:, 1:2, :].to_broadcast([128, ns, hd // 2]), ALU.mult, [ak, rk], [bk])
                    P.tt('dve', a[:, 0:nw].rearrange("p (g n) -> p g n", n=hd), a[:, 0:nw].rearrange("p (g n) -> p g n", n=hd),
                         rp[:, 0:hd].unsqueeze(1).to_broadcast([128, ns, hd]), ALU.mult, [ak, rk], [ak])
                    s, sk = staging()
                    sv = s[:].bitcast(BF16)[:, 0:nw]
                    P.tt('dve', sv, a[:, 0:nw], b[:, 0:nw], ALU.add, [ak, bk], [sk])
                    store_rows(dst, t, nb - c_off, nw, sv, sk)
                return ep

            if la in (1, 3):
                ropet = [P.sb("rope%d" % i, [128, 128], F32) for i in range(2)]

            def linear_tm_rope(W, n0, n1, epilogue):
                for nb in range(n0, n1, 512):
                    nw = min(512, n1 - nb)
                    wt, wk = load_w(W, 0, 8, nb, nw)
                    for t in range(NT):
                        ps, pk = C.psum()
                        for c in range(8):
                            P.mm(ps[:, 0:nw], xT[:, c, t * 128:(t + 1) * 128], wt[:, c, 0:nw], c == 0, c == 7, [('xT', t), wk], [pk])
                        P.dma('sp', ropet[t % 2][:], I["rope"][t * 128:(t + 1) * 128, :], (), [('rope', t % 2)])
                        epilogue(t, nb, nw, ps, pk)

            if la == 0:
                linear_tm(W, 8, 0, 512, ep_copy(O["q"], 0, False, scale=128 ** -0.5))
                linear_tm(W, 8, 512, 1024, ep_copy(O["k"], 512, False))
                linear_tm(W, 8, 1024, 2048, ep_copy(O["v"], 1024, True))
                linear_tm(W, 8, 2048, 3072, ep_copy(O["sr"], 2048, True, func=AF.Silu))
                zT = [P.sb("zT%d" % i, [16, TOK], BF16) for i in range(2)]
                wgs = [P.sb("wgs%d" % i, [16, 512], BF16) for i in range(2)]
                bgs = [P.sb("bgs%d" % i, [1, 512], BF16) for i in range(2)]
                ones = P.sb("ones", [1, 128], BF16)
                P.memset('pool', ones[:], 1.0, ['ones'])
                wz, wzk = load_w(W, 0, 8, 3072, 32)
                for i, (wn, bn) in enumerate((("wgf", "bgf"), ("wgb", "bgb"))):
                    P.dma('pool', wgs[i][:], I[wn][:, :], (), [('wgs', i)])
                    P.dma('pool', bgs[i][:], I[bn][:, :], (), [('bgs', i)])
                for i in range(2):
                    for tb in range(4):
                        ps, pk = C.psum()
                        for k in range(8):
                            P.mm(ps[0:16, :], wz[:, k, 16 * i:16 * i + 16], xT[:, k, tb * 512:(tb + 1) * 512], k == 0, k == 7,
                                 [('xT', t) for t in range(4 * tb, 4 * tb + 4)] + [wzk], [pk])
                        P.cp('act', zT[i][:, tb * 512:(tb + 1) * 512], ps[0:16, :], [pk], [('zT', i, tb)])
                for i, dst in enumerate((O["laf"], O["lab"])):
                    for t in range(NT):
                        ps, pk = C.psum()
                        P.mm(ps[:], zT[i][:, t * 128:(t + 1) * 128], wgs[i][:], True, False, [('zT', i, t // 4), ('wgs', i)], [pk])
                        P.mm(ps[:], ones[:], bgs[i][:], False, True, ['ones', ('bgs', i)], [pk])
                        s, sk = staging()
                        P.act(s[:], ps[:], AF.Exp, [pk], [sk], scale=-1.0)
                        P.act(s[:], s[:], AF.Ln, [sk], [sk], bias=1.0)
                        P.ts('dve', s[:], s[:], -1.0 / 16.0, ALU.mult, [sk], [sk])
                        store_rows(dst, t, 0, 512, s[:], sk)
            elif la == 1:
                load_gain(0, I["gq_rep"])
                linear_tm_rope(W, 0, 1024, make_ep_normrope(O["q"], 0, 0))
                load_gain(0, I["gk_rep"])
                linear_tm_rope(W, 1024, 2048, make_ep_normrope(O["k"], 1024, 0))
                linear_tm(W, 8, 2048, 3072, ep_copy(O["v"], 2048, True))
            elif la == 2:
                linear_tm(W, 8, 0, 2048, ep_copy(O["sz"], 0, True, func=AF.Silu))
                linear_tm(W, 8, 2048, 6144, ep_copy(O["xbc"], 2048, True))
                dtb = P.sb("dtb", [128, 64], F32)
                P.dma('sp', dtb[:], I["dtb"][0:1, :].partition_broadcast(128), (), ['dtb'])

                def ep_dt(t, nb, nw, ps, pk):
                    s, sk = staging()
                    P.tt('dve', s[:, 0:64], ps[:, 0:64], dtb[:], ALU.add, [pk, 'dtb'], [sk])
                    P.act(s[:, 0:64], s[:, 0:64], AF.Exp, [sk], [sk])
                    P.act(s[:, 0:64], s[:, 0:64], AF.Ln, [sk], [sk], bias=1.0)
                    store_rows(O["dt"], t, 0, 64, s[:, 0:64], sk)
                linear_tm(W, 8, 6144, 6208, ep_dt)
            elif la == 3:
                for part, dst, gn in ((0, O["q"], "gq_rep"), (1, O["k"], "gk_rep")):
                    for g in range(3):
                        load_gain(0, I[gn], D, g * D)
                        linear_tm_rope(W, part * 3072 + g * D, part * 3072 + (g + 1) * D,
                                       make_ep_normrope(dst, part * 3072, 0))
                linear_tm(W, 8, 6144, 9216, ep_copy(O["v"], 6144, True))
        P.finish()
        P.emit()
    return nc


def build_tri_masks(C, chunk=None):
    P = C.P
    M = {}
    for nm, cmp, base, cm, step in (("le", ALU.is_ge, 0, -1, 1), ("gt", ALU.is_gt, 0, 1, -1),
                                    ("ge", ALU.is_ge, 0, 1, -1), ("lt", ALU.is_gt, 0, -1, 1)):
        m = P.sb("mask_" + nm, [128, 128], F32)
        k = 'mask_' + nm
        P.memset('pool', m[:], 1.0, [k])
        P.asel(m[:], m[:], [[step, 128]], cmp, 0.0, base, cm, [k], [k])
        if chunk == 64:
            P.memset('pool', m[0:64, 64:128], 0.0, [k])
            P.memset('pool', m[64:128, 0:64], 0.0, [k])
        M[nm] = (m, k)
    return M


def build_b0():
    nc = bass.Bass("TRN2", target_bir_lowering=False)
    qT = _din(nc, "qT", [128, SEQ]); kT = _din(nc, "kT", [128, SEQ])
    kk = _din(nc, "k", [SEQ, 128]); vv = _din(nc, "v", [SEQ, 256], BF16)
    la = {"f": _din(nc, "laf", [SEQ, 128]), "b": _din(nc, "lab", [SEQ, 128])}
    oo = {"f": _dout(nc, "of", [SEQ, 256]), "b": _dout(nc, "ob", [SEQ, 256])}
    with ExitStack() as es:
        C = Ctx(nc, es)
        P = C.P
        M = build_tri_masks(C, 64)
        NB = 3
        T = {}
        for d in "fb":
            T[d] = dict(
                qT=[P.sb("qT%s%d" % (d, i), [128, 128], F32) for i in range(NB)],
                kT=[P.sb("kT%s%d" % (d, i), [128, 128], F32) for i in range(NB)],
                k=[P.sb("k%s%d" % (d, i), [128, 128], F32) for i in range(NB)],
                v=[P.sb("v%s%d" % (d, i), [128, 256], BF16) for i in range(NB)],
                la=[P.sb("la%s%d" % (d, i), [128, 128], F32) for i in range(NB)],
                E=[P.sb("E%s%d" % (d, i), [128, 128], F32) for i in range(2)],
                Ei=[P.sb("Ei%s%d" % (d, i), [128, 128], F32) for i in range(2)],
                Es=[P.sb("Es%s%d" % (d, i), [128, 128], F32) for i in range(2)],
                qin=[P.sb("qin%s%d" % (d, i), [128, 128], BF16) for i in range(2)],
                kout=[P.sb("kout%s%d" % (d, i), [128, 128], BF16) for i in range(2)],
                kst=[P.sb("kst%s%d" % (d, i), [128, 128], BF16) for i in range(2)],
                sc=[P.sb("sc%s%d" % (d, i), [128, 128], BF16) for i in range(2)],
                o=[P.sb("o%s%d" % (d, i), [128, 256], F32) for i in range(2)],
                S=P.sb("S%s" % d, [128, 256], F32),
                Sb=[P.sb("Sb%s%d" % (d, i), [128, 256], BF16) for i in range(2)],
            )
            P.memset('pool', T[d]["S"][:], 0.0, [('S', d)])
            P.memset('pool', T[d]["Sb"][0][:], 0.0, [('Sb', d, 0)])
            T[d]["sbi"] = 0
        NTL = SEQ // 128
        for idx in range(NTL):
            for d in "fb":
                t = idx if d == "f" else NTL - 1 - idx
                R = T[d]
                i3 = P.rr('in' + d, NB)
                i2 = P.rr('w' + d, 2)
                kq, kkT, kkk, kv, kla = [(n, d, i3) for n in ("qT", "kT", "k", "v", "la")]
                P.dma('sp', R["qT"][i3][:], qT[:, t * 128:(t + 1) * 128], (), [kq])
                P.dma('sp', R["kT"][i3][:], kT[:, t * 128:(t + 1) * 128], (), [kkT])
                P.dma('sp', R["k"][i3][:], kk[t * 128:(t + 1) * 128, :], (), [kkk])
                P.dma('sp', R["v"][i3][:], vv[t * 128:(t + 1) * 128, :], (), [kv])
                P.dma('sp', R["la"][i3][:], la[d][t * 128:(t + 1) * 128, :], (), [kla])
                minc, minck = M["le"] if d == "f" else M["ge"]
                mexc, mexck = M["gt"] if d == "f" else M["lt"]
                msc, msck = M["le"] if d == "f" else M["gt"]
                pb, pbk = C.psum()
                P.mm(pb[:, 0:128], R["la"][i3][:], minc[:], True, True, [kla, minck], [pbk])
                psf, psfk = C.psum()
                P.mm(psf[:, 0:128], mexc[:], R["la"][i3][:], True, True, [kla, mexck], [psfk])
                E, Ei, Es = R["E"][i2], R["Ei"][i2], R["Es"][i2]
                kE, kEi, kEs = ('E', d, i2), ('Ei', d, i2), ('Es', d, i2)
                P.act(E[:], pb[:, 0:128], AF.Exp, [pbk], [kE])
                P.act(Ei[:], pb[:, 0:128], AF.Exp, [pbk], [kEi], scale=-1.0)
                P.act(Es[:], psf[:, 0:128], AF.Exp, [psfk], [kEs])
                qin, kout, kst, sc = R["qin"][i2], R["kout"][i2], R["kst"][i2], R["sc"][i2]
                kqin, kkout, kkst, ksc = ('qin', d, i2), ('kout', d, i2), ('kst', d, i2), ('sc', d, i2)
                P.tt('dve', qin[:], R["qT"][i3][:], E[:], ALU.mult, [kq, kE], [kqin])
                P.tt('pool', kout[:], R["kT"][i3][:], Ei[:], ALU.mult, [kkT, kEi], [kkout])
                P.tt('pool', kst[:], R["k"][i3][:], Es[:], ALU.mult, [kkk, kEs], [kkst])
                psc, psck = C.psum()
                P.mm(psc[:, 0:128], kout[:], qin[:], True, True, [kkout, kqin], [psck])
                P.tt('dve', sc[:], psc[:, 0:128], msc[:], ALU.mult, [psck, msck], [ksc])
                po, pok = C.psum()
                P.mm(po[:, 0:256], sc[:], R["v"][i3][:], True, False, [ksc, kv], [pok])
                order = (0, 1) if d == "f" else (1, 0)
                for ci, c in enumerate(order):
                    sbi = R["sbi"]
                    P.mm(po[64 * c:64 * c + 64, 0:256], qin[:, 64 * c:64 * c + 64], R["Sb"][sbi][:], False, ci == 1,
                         [kqin, ('Sb', d, sbi)], [pok])
                    pu, puk = C.psum()
                    P.mm(pu[:, 0:256], kst[64 * c:64 * c + 64, :], R["v"][i3][64 * c:64 * c + 64, :], True, True, [kkst, kv], [puk])
                    dcol = (64 * c + 63) if d == "f" else (64 * c)
                    P.stt(R["S"][:], R["S"][:], E[:, dcol:dcol + 1], pu[:, 0:256], ALU.mult, ALU.add,
                          [('S', d), kE, puk], [('S', d)])
                    nsb = 1 - sbi
                    P.cp('act', R["Sb"][nsb][:], R["S"][:], [('S', d)], [('Sb', d, nsb)])
                    R["sbi"] = nsb
                o, ok = R["o"][i2], ('o', d, i2)
                P.cp('act', o[:], po[:, 0:256], [pok], [ok])
                P.dma('sp', oo[d][t * 128:(t + 1) * 128, :], o[:], [ok], ())
        P.finish()
        P.emit()
    return nc


def _c(a):
    return np.ascontiguousarray(a)


def rope_table(hd=64):
    half = hd // 2
    inv = (10000.0 ** (-np.arange(half, dtype=np.float32) * 2.0 / hd)).astype(np.float32)
    ang = np.arange(SEQ, dtype=np.float32)[:, None] * inv[None, :]
    cos = np.cos(ang).astype(np.float32)
    sin = np.sin(ang).astype(np.float32)
    return _c(np.concatenate([cos, cos, -sin, sin], axis=1))


def tok_slice(c):
    b, q = c // 4, c % 4
    return b, slice(q * TOK, (q + 1) * TOK)


def common_c_inputs(W, lc, c):
    b, sl = tok_slice(c)
    return {"norm_ffn": W["norm_ffn"][lc:lc + 1], "ffn_w_in": W["ffn_w_in"][lc], "ffn_w_out": W["ffn_w_out"][lc],
            "ple_norm": W["ple_norm"][lc:lc + 1], "ple_w_gate": W["ple_w_gate"][lc], "ple_w_proj": W["ple_w_proj"][lc],
            "pT": _c(W["p"][lc, b, sl, :].T)}


def build_b1(lam_init, nunits=2, seq=SEQ):
    nc = bass.Bass("TRN2", target_bir_lowering=False)
    qT = _din(nc, "qT", [nunits * 128, seq], BF16)
    kT = _din(nc, "kT", [nunits * 128, seq], BF16)
    vv = _din(nc, "v", [nunits * seq, 128], BF16)
    lamv = _din(nc, "lamv", [1, 256])
    oo = _dout(nc, "o", [nunits * seq, 128])
    NKT = seq // 128
    NQB = seq // 512
    with ExitStack() as es:
        P = Prog(nc, es)
        pS = [P.ps("pS%d" % i, [128, 512], F32) for i in range(3)]
        pO = [P.ps("pO%d" % i, [128, 512], F32) for i in range(4)]
        pL = P.ps("pL", [128, 512], F32)
        qs = [P.sb("qs%d" % i, [128, seq], BF16) for i in range(2)]
        ks = [P.sb("ks%d" % i, [128, seq], BF16) for i in range(2)]
        vs = [P.sb("vs%d" % i, [128, NKT, 130], BF16) for i in range(2)]
        eT = [P.sb("eT%d" % i, [128, 512], BF16) for i in range(3)]
        ob = [P.sb("ob%d" % i, [128, 128], F32) for i in range(4)]
        o2 = [P.sb("o2%d" % i, [128, 128], F32) for i in range(2)]
        rl = [P.sb("rl%d" % i, [128, 4], F32) for i in range(4)]
        lt = P.sb("lt", [1, 256], F32)
        lw = P.sb("lw", [1, 8], F32)
        ones = P.sb("ones", [1, 128], F32)
        lamb = P.sb("lamb", [128, 2], F32)
        P.dma('sp', lt[:], lamv[:, :], (), ['lt'])
        P.memset('pool', ones[:], 1.0, ['ones'])
        P.tt('dve', lt[:, 0:64], lt[:, 0:64], lt[:, 64:128], ALU.mult, ['lt'], ['lt'])
        P.tt('dve', lt[:, 128:192], lt[:, 128:192], lt[:, 192:256], ALU.mult, ['lt'], ['lt'])
        P.red(lw[:, 0:1], lt[:, 0:64], ['lt'], ['lw'])
        P.red(lw[:, 1:2], lt[:, 128:192], ['lt'], ['lw'])
        P.act(lw[:, 0:2], lw[:, 0:2], AF.Exp, ['lw'], ['lw'])
        P.tt('dve', lw[:, 2:3], lw[:, 0:1], lw[:, 1:2], ALU.subtract, ['lw'], ['lw'])
        P.ts('dve', lw[:, 4:6], lw[:, 2:3].to_broadcast([1, 2]), lam_init, ALU.add, ['lw'], ['lw'])
        P.mm(pL[:, 0:2], ones[:], lw[:, 4:6], True, True, ['ones', 'lw'], ['pL'])
        P.cp('dve', lamb[:], pL[:, 0:2], ['pL'], ['lamb'])
        scale = 64 ** -0.5
        for u in range(nunits):
            bi = u % 2
            for cb in range(4):
                cs = slice(cb * (seq // 4), (cb + 1) * (seq // 4))
                P.dma('sp', qs[bi][:, cs], qT[u * 128:(u + 1) * 128, cs], (), [('qs', bi, cb)])
                P.dma('sp', ks[bi][:, cs], kT[u * 128:(u + 1) * 128, cs], (), [('ks', bi, cb)])
                kts = slice(cb * (NKT // 4), (cb + 1) * (NKT // 4))
                P.dma('sp', vs[bi][:, kts, 0:128],
                      vv[u * seq + cb * (seq // 4):u * seq + (cb + 1) * (seq // 4), :].rearrange("(t p) d -> p t d", p=128),
                      (), [('vs', bi, cb)])
            P.memset('pool', vs[bi][:, :, 128:130], 1.0, [('vones', bi)])
            steps = [(qb, kt, s) for qb in range(NQB) for kt in range(NKT) for s in range(2)]

            def emit_S(i):
                qb, kt, s = steps[i]
                j = i % 3
                P.mm(pS[j][:], ks[bi][64 * s:64 * s + 64, kt * 128:(kt + 1) * 128], qs[bi][64 * s:64 * s + 64, qb * 512:(qb + 1) * 512],
                     True, True, [('ks', bi, kt // (NKT // 4)), ('qs', bi, qb // (NQB // 4))], [('pS', j)])
                P.act(eT[j][:], pS[j][:], AF.Exp, [('pS', j)], [('eT', j)], scale=scale)

            def emit_AV(i):
                qb, kt, s = steps[i]
                j = i % 3
                for jq in range(4):
                    bank = s * 2 + jq // 2
                    col = (jq % 2) * 256
                    P.mm(pO[bank][:, col:col + 129], eT[j][:, jq * 128:(jq + 1) * 128], vs[bi][:, kt, 0:129],
                         (kt == 0 and jq % 2 == 0), (kt == NKT - 1),
                         [('eT', j), ('vs', bi, kt // (NKT // 4)), ('vones', bi)], [('pO', bank)])
                if kt == NKT - 1 and s == 1:
                    for jq in range(4):
                        b0, b1 = pO[jq // 2], pO[2 + jq // 2]
                        k0, k1 = ('pO', jq // 2), ('pO', 2 + jq // 2)
                        col = (jq % 2) * 256
                        r = rl[P.rr('rl', 4)]
                        rk = ('rl', id(r))
                        P.recip(r[:, 0:1], b0[:, col + 128:col + 129], [k0], [rk])
                        P.recip(r[:, 1:2], b1[:, col + 128:col + 129], [k1], [rk])
                        P.tt('dve', r[:, 2:3], r[:, 1:2], lamb[:, 0:1], ALU.mult, [rk, 'lamb'], [rk])
                        oi = P.rr('ob', 4)
                        o2i = P.rr('o2', 2)
                        P.ts('dve', ob[oi][:], b0[:, col:col + 128], r[:, 0:1], ALU.mult, [k0, rk], [('ob', oi)])
                        P.ts('dve', o2[o2i][:], b1[:, col:col + 128], r[:, 2:3], ALU.mult, [k1, rk], [('o2', o2i)])
                        P.tt('pool', ob[oi][:], ob[oi][:], o2[o2i][:], ALU.subtract, [('ob', oi), ('o2', o2i)], [('ob', oi)])
                        row = u * seq + qb * 512 + jq * 128
                        P.dma('sp', oo[row:row + 128, :], ob[oi][:], [('ob', oi)], ())

            n = len(steps)
            emit_S(0)
            emit_S(1)
            for i in range(n):
                if i + 2 < n:
                    emit_S(i + 2)
                emit_AV(i)
        P.finish()
        P.emit()
    return nc


DIL_LIST = (1, 4, 16)
KPAD = 10240


def build_b3(units=tuple(g for g in range(3) for _ in range(4))):
    nc = bass.Bass("TRN2", target_bir_lowering=False)
    NU = len(units)
    qT = _din(nc, "qT", [NU * 64, SEQ], BF16)
    kT = _din(nc, "kT", [NU * 64, KPAD], BF16)
    vp = _din(nc, "vp", [NU * KPAD, 64], BF16)
    oo = _dout(nc, "o", [NU * SEQ, 65])
    with ExitStack() as es:
        C = Ctx(nc, es)
        P = C.P
        M = build_tri_masks(C, None)
        mk = [P.sb("m3_%d" % i, [128, 256], F32) for i in range(3)]
        for i in range(3):
            P.cp('pool', mk[i][:, 0:128], M["ge"][0][:], [M["ge"][1]], [('m3', i)])
            P.cp('pool', mk[i][:, 128:256], M["le"][0][:], [M["le"][1]], [('m3', i)])
        P.memset('pool', mk[1][0:64, 0:128], 0.0, [('m3', 1)])
        P.memset('pool', mk[2][64:128, 128:256], 0.0, [('m3', 2)])
        qs = [P.sb("qs%d" % i, [64, SEQ], BF16) for i in range(2)]
        ks = [P.sb("ks%d" % i, [64, KPAD], BF16) for i in range(2)]
        NVB = 4
        va = [P.sb("va%d" % i, [128, 2, 66], BF16) for i in range(NVB)]
        for i in range(NVB):
            P.memset('pool', va[i][:, :, 64:66], 1.0, [('vones', i)])
        ef = [P.sb("ef%d" % i, [128, 256], F32) for i in range(3)]
        em = [P.sb("em%d" % i, [128, 256], BF16) for i in range(3)]
        osb = [P.sb("osb%d" % i, [128, 65], F32) for i in range(3)]
        scale = 64 ** -0.5
        steps = []
        for u, g in enumerate(units):
            dil = DIL_LIST[g]
            L = SEQ // dil
            for r in range(dil):
                for blk in range(L // 128):
                    mt = 1 if blk == 0 else (2 if blk == L // 128 - 1 else 0)
                    steps.append((u, r * L + blk * 128, r * (L + 128) + blk * 128, mt))
        loaded = set()

        def ensure_unit(u):
            if u in loaded:
                return
            loaded.add(u)
            bi = u % 2
            for cb in range(2):
                P.dma('sp', qs[bi][:, cb * 4096:(cb + 1) * 4096], qT[u * 64:(u + 1) * 64, cb * 4096:(cb + 1) * 4096], (), [('qs', bi, cb)])
                P.dma('sp', ks[bi][:, cb * 5120:(cb + 1) * 5120], kT[u * 64:(u + 1) * 64, cb * 5120:(cb + 1) * 5120], (), [('ks', bi, cb)])

        def kkeys(bi, w0):
            s = {w0 // 5120, (w0 + 255) // 5120}
            return [('ks', bi, x) for x in s]

        def emit_S(i):
            u, q0, w0, mt = steps[i]
            ensure_unit(u)
            bi = u % 2
            j = i % 3
            ps, pk = C.psf[j], ('psf', j)
            rd = kkeys(bi, w0) + [('qs', bi, q0 // 4096)]
            P.mm(ps[:, 0:128], ks[bi][:, w0:w0 + 128], qs[bi][:, q0:q0 + 128], True, False, rd, [pk])
            P.mm(ps[:, 128:256], ks[bi][:, w0 + 128:w0 + 256], qs[bi][:, q0:q0 + 128], False, True, rd, [pk])
            P.act(ef[j][:], ps[:, 0:256], AF.Exp, [pk], [('ef', j)], scale=scale)
            P.tt('dve', em[j][:], ef[j][:], mk[mt][:], ALU.mult, [('ef', j), ('m3', mt)], [('em', j)])
            vi = i % NVB
            P.dma('sp', va[vi][:, :, 0:64], vp[u * KPAD + w0:u * KPAD + w0 + 256, :].rearrange("(t p) d -> p t d", p=128), (), [('va', vi)])

        def emit_AV(i):
            u, q0, w0, mt = steps[i]
            j = i % 3
            vi = i % NVB
            po, pok = C.psf[3 + j], ('psf', 3 + j)
            P.mm(po[:, 0:65], em[j][:, 0:128], va[vi][:, 0, 0:65], True, False, [('em', j), ('va', vi), ('vones', vi)], [pok])
            P.mm(po[:, 0:65], em[j][:, 128:256], va[vi][:, 1, 0:65], False, True, [('em', j), ('va', vi), ('vones', vi)], [pok])
            P.cp('act', osb[j][:], po[:, 0:65], [pok], [('osb', j)])
            P.dma('sp', oo[u * SEQ + q0:u * SEQ + q0 + 128, :], osb[j][:], [('osb', j)], ())

        n = len(steps)
        emit_S(0)
        emit_S(1)
        for i in range(n):
            if i + 2 < n:
                emit_S(i + 2)
            emit_AV(i)
        P.finish()
        P.emit()
    return nc


def build_b2(nunits=2, seq=SEQ):
    nc = bass.Bass("TRN2", target_bir_lowering=False)
    SP = seq + 4
    xbcT = _din(nc, "xbcT", [nunits * 512, SP], BF16)
    cw = _din(nc, "cw", [nunits * 128, 20])
    cb = _din(nc, "cb", [nunits * 128, 4])
    dtin = _din(nc, "dt", [nunits * 128, (seq // 128) * 8])
    alog = _din(nc, "alog", [nunits, 8])
    dsk = _din(nc, "dsk", [nunits, 4])
    yo = {"f": _dout(nc, "yf", [nunits * seq, 256]), "b": _dout(nc, "yb", [nunits * seq, 256])}
    NCH = seq // 128
    TB = 2048
    with ExitStack() as es:
        C = Ctx(nc, es)
        P = C.P
        ident = C.ident
        M = build_tri_masks(C, None)
        NEG = {}
        for d, src in (("f", "le"), ("b", "gt")):
            t = P.sb("neg" + d, [128, 4, 128], F32)
            for hh in range(4):
                P.ts('dve', t[:, hh, :], M[src][0][:], 30000.0, ALU.mult, [M[src][1]], [('neg', d)], s2=-30000.0, op1=ALU.add)
            NEG[d] = t
        xa = P.sb("xa", [128, 4, TB + 4], BF16)
        xb = P.sb("xb", [128, 4, TB + 4], BF16)
        XC = P.sb("XC", [128, 4, seq], BF16)
        Dg = P.sb("Dg", [128, 20, 128], BF16)
        cwt = P.sb("cwt", [128, 20], F32)
        cbt = P.sb("cbt", [128, 4], F32)
        dts = P.sb("dts", [128, NCH, 8], F32)
        aneg = P.sb("aneg", [128, 8], F32)
        dskt = P.sb("dskt", [128, 4], F32)
        cbs = [P.sb("cbs%d" % i, [128, 128], F32) for i in range(2)]
        R = {}
        for d in "fb":
            R[d] = dict(
                xtm=[P.sb("xtm%s%d" % (d, i), [128, 384], BF16) for i in range(2)],
                dta=[P.sb("dta%s%d" % (d, i), [128, 8], F32) for i in range(2)],
                eac=[P.sb("eac%s%d" % (d, i), [128, 16], F32) for i in range(2)],
                nac=[P.sb("nac%s%d" % (d, i), [128, 4], F32) for i in range(2)],
                eal=[P.sb("eal%s%d" % (d, i), [128, 4], F32) for i in range(2)],
                T=[P.sb("T%s%d" % (d, i), [128, 4, 128], F32) for i in range(2)],
                WT=[P.sb("WT%s%d" % (d, i), [128, 4, 128], BF16) for i in range(2)],
                xdt=[P.sb("xdt%s%d" % (d, i), [128, 256], BF16) for i in range(2)],
                xw=[P.sb("xw%s%d" % (d, i), [128, 256], BF16) for i in range(2)],
                yoff=[P.sb("yoff%s%d" % (d, i), [128, 256], F32) for i in range(2)],
                ysb=[P.sb("ysb%s%d" % (d, i), [128, 256], F32) for i in range(2)],
                ST=P.sb("ST" + d, [128, 256], F32),
                STb=[P.sb("STb%s%d" % (d, i), [128, 256], BF16) for i in range(2)],
            )
            for i in range(2):
                P.memset('pool', R[d]["dta"][i][:], 0.0, [('dta', d, i)])
        for u in range(nunits):
            P.dma('sp', cwt[:], cw[u * 128:(u + 1) * 128, :], (), ['cwt'])
            P.dma('sp', cbt[:], cb[u * 128:(u + 1) * 128, :], (), ['cbt'])
            P.dma('sp', dts[:].rearrange("p t e -> p (t e)"), dtin[u * 128:(u + 1) * 128, :], (), ['dts'])
            P.dma('sp', aneg[:], alog[u:u + 1, :].partition_broadcast(128), (), ['aneg'])
            P.dma('sp', dskt[:], dsk[u:u + 1, :].partition_broadcast(128), (), ['dskt'])
            P.act(aneg[:], aneg[:], AF.Exp, ['aneg'], ['aneg'])
            P.ts('dve', aneg[:], aneg[:], -1.0, ALU.mult, ['aneg'], ['aneg'])
            for j in range(20):
                P.ts('dve', Dg[:, j, :], ident[:], cwt[:, j:j + 1], ALU.mult, ['ident', 'cwt'], ['Dg'])
            for blk in range(seq // TB):
                for ct in range(4):
                    r0 = u * 512 + ct * 128
                    P.dma('sp', xa[:, ct, :], xbcT[r0:r0 + 128, blk * TB:blk * TB + TB + 4], (), [('xa', ct)])
                    P.dma('sp', xb[:, ct, 0:TB + 3], xbcT[r0:r0 + 128, blk * TB + 1:blk * TB + TB + 4], (), [('xb', ct)])
                for ct in range(4):
                    for tb in range(TB // 512):
                        ps, pk = C.psum()
                        for w in range(5):
                            src = xa if w % 2 == 0 else xb
                            off = tb * 512 + (w if w % 2 == 0 else w - 1)
                            P.mm(ps[:], Dg[:, ct * 5 + w, :], src[:, ct, off:off + 512], w == 0, w == 4,
                                 ['Dg', ('xa', ct), ('xb', ct)], [pk])
                        g0 = blk * TB + tb * 512
                        P.act(XC[:, ct, g0:g0 + 512], ps[:], AF.Silu, [pk, 'cbt'], [('XC', g0 // 512)], bias=cbt[:, ct:ct + 1])
            for d in "fb":
                P.memset('pool', R[d]["ST"][:], 0.0, [('ST', d)])
                P.memset('pool', R[d]["STb"][0][:], 0.0, [('STb', d, 0)])
                R[d]["si"] = 0
            for idx in range(NCH):
                for d in "fb":
                    n = idx if d == "f" else NCH - 1 - idx
                    Q = R[d]
                    i2 = P.rr('r' + d, 2)
                    t0 = n * 128
                    xk = [('XC', n // 4)]
                    pc, pck = C.psum()
                    P.mm(pc[:, 0:128], XC[:, 2, t0:t0 + 128], XC[:, 3, t0:t0 + 128], True, True, xk, [pck])
                    ci2 = P.rr('cbs', 2)
                    P.cp('act', cbs[ci2][:], pc[:, 0:128], [pck], [('cbs', ci2)])
                    pb, pbk = C.psumb()
                    for ct in range(3):
                        P.tr(pb[:, ct * 128:(ct + 1) * 128], XC[:, ct, t0:t0 + 128], ident[:], xk + ['ident'], [pbk])
                    xtm, kx = Q["xtm"][i2], ('xtm', d, i2)
                    P.cp('act', xtm[:], pb[:, 0:384], [pbk], [kx])
                    dcol = 0 if d == "f" else 4
                    dta, kdta = Q["dta"][i2], ('dta', d, i2)
                    P.tt('dve', dta[:, 0:4], dts[:, n, dcol:dcol + 4], aneg[:, dcol:dcol + 4], ALU.mult, ['dts', 'aneg'], [kdta])
                    minc, minck = M["le"] if d == "f" else M["ge"]
                    mexc, mexck = M["gt"] if d == "f" else M["lt"]
                    pa, pak = C.psum()
                    P.mm(pa[:, 0:8], minc[:], dta[:], True, False, [minck, kdta], [pak])
                    P.mm(pa[:, 8:16], mexc[:], dta[:], False, True, [mexck, kdta], [pak])
                    pr, prk = C.psum()
                    for hh in range(4):
                        P.mm(pr[:, hh * 128:(hh + 1) * 128], dta[:, hh:hh + 1].to_broadcast([128, 128]), minc[:], hh == 0, hh == 3,
                             [kdta, minck], [prk])
                    eac, keac = Q["eac"][i2], ('eac', d, i2)
                    nac, knac = Q["nac"][i2], ('nac', d, i2)
                    eal, keal = Q["eal"][i2], ('eal', d, i2)
                    P.act(eac[:], pa[:, 0:16], AF.Exp, [pak], [keac])
                    P.ts('dve', nac[:], pa[:, 0:4], -1.0, ALU.mult, [pak], [knac])
                    lcol = 127 if d == "f" else 0
                    P.cp('dve', eal[:], pr[:].rearrange("p (h n) -> p h n", n=128)[:, :, lcol], [prk], [keal])
                    P.act(eal[:], eal[:], AF.Exp, [keal], [keal])
                    Tt, kT = Q["T"][i2], ('T', d, i2)
                    P.tt('dve', Tt[:].rearrange("p h n -> p (h n)"), pr[:], NEG[d][:].rearrange("p h n -> p (h n)"), ALU.add,
                         [prk, ('neg', d)], [kT])
                    for hh in range(4):
                        P.act(Tt[:, hh, :], Tt[:, hh, :], AF.Exp, [kT, knac], [kT], bias=nac[:, hh:hh + 1])
                    WT, kWT = Q["WT"][i2], ('WT', d, i2)
                    P.tt('dve', WT[:], Tt[:], cbs[ci2][:].unsqueeze(1).to_broadcast([128, 4, 128]), ALU.mult, [kT, ('cbs', ci2)], [kWT])
                    xdt, kxdt = Q["xdt"][i2], ('xdt', d, i2)
                    xw, kxw = Q["xw"][i2], ('xw', d, i2)
                    P.tt('pool', xdt[:].rearrange("p (h n) -> p h n", n=64), xtm[:, 0:256].rearrange("p (h n) -> p h n", n=64),
                         dts[:, n, dcol:dcol + 4].unsqueeze(2).to_broadcast([128, 4, 64]), ALU.mult, [kx, 'dts'], [kxdt])
                    P.tt('pool', xw[:].rearrange("p (h n) -> p h n", n=64), xdt[:].rearrange("p (h n) -> p h n", n=64),
                         eac[:, 8:12].unsqueeze(2).to_broadcast([128, 4, 64]), ALU.mult, [kxdt, keac], [kxw])
                    pyd, pydk = C.psum()
                    for hh in range(4):
                        P.mm(pyd[:, hh * 64:(hh + 1) * 64], WT[:, hh, :], xdt[:, hh * 64:(hh + 1) * 64], hh == 0, hh == 3, [kWT, kxdt], [pydk])
                    si = Q["si"]
                    pyo, pyok = C.psum()
                    P.mm(pyo[:, 0:256], XC[:, 3, t0:t0 + 128], Q["STb"][si][:], True, True, xk + [('STb', d, si)], [pyok])
                    pst, pstk = C.psum()
                    P.mm(pst[:, 0:256], xtm[:, 256:384], xw[:], True, True, [kx, kxw], [pstk])
                    yoff, kyoff = Q["yoff"][i2], ('yoff', d, i2)
                    P.tt('dve', yoff[:].rearrange("p (h n) -> p h n", n=64), pyo[:, 0:256].rearrange("p (h n) -> p h n", n=64),
                         eac[:, 0:4].unsqueeze(2).to_broadcast([128, 4, 64]), ALU.mult, [pyok, keac], [kyoff])
                    ysb, kysb = Q["ysb"][i2], ('ysb', d, i2)
                    P.tt('dve', ysb[:], pyd[:, 0:256], yoff[:], ALU.add, [pydk, kyoff], [kysb])
                    if d == "f":
                        P.tt('pool', yoff[:].rearrange("p (h n) -> p h n", n=64), xtm[:, 0:256].rearrange("p (h n) -> p h n", n=64),
                             dskt[:].unsqueeze(2).to_broadcast([128, 4, 64]), ALU.mult, [kx, 'dskt', kyoff], [kyoff])
                        P.tt('pool', ysb[:], ysb[:], yoff[:], ALU.add, [kysb, kyoff], [kysb])
                    P.dma('sp', yo[d][u * seq + t0:u * seq + t0 + 128, :], ysb[:], [kysb], ())
                    for hh in range(4):
                        P.stt(Q["ST"][:, hh * 64:(hh + 1) * 64], Q["ST"][:, hh * 64:(hh + 1) * 64], eal[:, hh:hh + 1],
                              pst[:, hh * 64:(hh + 1) * 64], ALU.mult, ALU.add, [('ST', d), keal, pstk], [('ST', d)])
                    nsi = 1 - si
                    P.cp('act', Q["STb"][nsi][:], Q["ST"][:], [('ST', d)], [('STb', d, nsi)])
                    Q["si"] = nsi
                    if d == "b" and idx % 4 == 0:
                        P.barrier()
        P.finish()
        P.emit()
    return nc


_DEBUG = {}


def _run(nc, ins):
    res = run_bass_kernel_spmd(nc, ins, core_ids=list(range(8)))
    return res.results


def _gather(r, name):
    a = np.concatenate([np.asarray(r[c][name]) for c in range(8)], axis=0)
    return a.reshape(2, SEQ, a.shape[-1])


def stage_a0(W, x):
    ins = []
    for c in range(8):
        b, sl = tok_slice(c)
        ins.append({"h": _c(x[b, sl]), "norm_mix": W["norm_mix"][0:1], "w_in": W["gla_w_in"][0],
                    "wgf": W["gla_w_gate_f"][0], "wgb": W["gla_w_gate_b"][0],
                    "bgf": W["gla_b_gate_f"][0:1], "bgb": W["gla_b_gate_b"][0:1]})
    r = _run(build_ac(None, 0), ins)
    return {n: _gather(r, m) for n, m in (("h", "h_out"), ("q", "q"), ("k", "k"), ("v", "v"), ("sr", "sro"), ("laf", "laf"), ("lab", "lab"))}


def stage_b0(A):
    ins = []
    for c in range(8):
        b, hd = c // 4, c % 4
        hs = slice(hd * 128, (hd + 1) * 128)
        ins.append({"qT": _c(A["q"][b][:, hs].T), "kT": _c(A["k"][b][:, hs].T), "k": _c(A["k"][b][:, hs]),
                    "v": _c(A["v"][b][:, hd * 256:(hd + 1) * 256]), "laf": _c(A["laf"][b][:, hs]), "lab": _c(A["lab"][b][:, hs])})
    r = _run(build_b0(), ins)
    of = np.zeros((2, SEQ, 1024), np.float32)
    ob = np.zeros((2, SEQ, 1024), np.float32)
    for c in range(8):
        b, hd = c // 4, c % 4
        of[b][:, hd * 256:(hd + 1) * 256] = r[c]["of"]
        ob[b][:, hd * 256:(hd + 1) * 256] = r[c]["ob"]
    return of, ob


def stage_ac01(W, A, of, ob, rope):
    ins = []
    for c in range(8):
        b, sl = tok_slice(c)
        d = {"h": _c(A["h"][b, sl]), "of": _c(of[b, sl]), "ob": _c(ob[b, sl]), "sr": _c(A["sr"][b, sl]),
             "g_rep": _c(np.tile(W["gla_g_out"][0], 4)[None, :]), "w_out": W["gla_w_out"][0],
             "norm_mix": W["norm_mix"][1:2], "w_in": W["diff_w_in"][0],
             "gq_rep": _c(np.tile(W["diff_g_q"][0].reshape(-1), 8)[None, :]),
             "gk_rep": _c(np.tile(W["diff_g_k"][0].reshape(-1), 8)[None, :]), "rope": _c(rope[sl])}
        d.update(common_c_inputs(W, 0, c))
        ins.append(d)
    r = _run(build_ac(0, 1), ins)
    return {n: _gather(r, m) for n, m in (("h", "h_out"), ("q", "q"), ("k", "k"), ("v", "v"))}


def stage_b1(W, A):
    lam_init = 0.8 - 0.6 * math.exp(-0.3 * 1)
    lamv = _c(np.concatenate([W["diff_lam_q1"][0], W["diff_lam_k1"][0], W["diff_lam_q2"][0], W["diff_lam_k2"][0]])[None, :])
    ins = []
    for c in range(8):
        b = c // 4
        hds = (2 * (c % 4), 2 * (c % 4) + 1)
        ins.append({"qT": _c(np.concatenate([A["q"][b][:, h * 128:(h + 1) * 128].T for h in hds], axis=0)),
                    "kT": _c(np.concatenate([A["k"][b][:, h * 128:(h + 1) * 128].T for h in hds], axis=0)),
                    "v": _c(np.concatenate([A["v"][b][:, h * 128:(h + 1) * 128] for h in hds], axis=0)),
                    "lamv": lamv})
    r = _run(build_b1(lam_init), ins)
    on = np.zeros((2, SEQ, 1024), np.float32)
    for c in range(8):
        b = c // 4
        for u in range(2):
            h = 2 * (c % 4) + u
            on[b][:, h * 128:(h + 1) * 128] = r[c]["o"][u * SEQ:(u + 1) * SEQ]
    return on


def stage_ac12(W, A, on):
    ins = []
    for c in range(8):
        b, sl = tok_slice(c)
        d = {"h": _c(A["h"][b, sl]), "on": _c(on[b, sl]), "g_rep": _c(np.tile(W["diff_g_sub"][0], 8)[None, :]),
             "w_out": W["diff_w_out"][0], "norm_mix": W["norm_mix"][2:3], "w_in": W["ssd_w_in"][0],
             "dtb": _c(np.concatenate([W["ssd_dt_bias_f"][0], W["ssd_dt_bias_b"][0]])[None, :])}
        d.update(common_c_inputs(W, 1, c))
        ins.append(d)
    r = _run(build_ac(1, 2), ins)
    return {n: _gather(r, m) for n, m in (("h", "h_out"), ("sz", "szo"), ("xbc", "xbc"), ("dt", "dt"))}


def stage_b2(W, A):
    ins = []
    cwf = W["ssd_conv_w"][0]
    cbf = W["ssd_conv_b"][0]
    for c in range(8):
        b = c // 4
        xs, cws, cbs_, cbr, dts, als, dks = [], [], [], [], [], [], []
        for u in range(2):
            g = 2 * (c % 4) + u
            sel = np.r_[g * 256:(g + 1) * 256, 2048 + g * 128:2048 + (g + 1) * 128, 3072 + g * 128:3072 + (g + 1) * 128]
            xt = A["xbc"][b][:, sel].T
            xs.append(np.pad(xt, ((0, 0), (2, 2))))
            cws.append(cwf[:, sel].T.reshape(4, 128, 5).transpose(1, 0, 2).reshape(128, 20))
            cbs_.append(cbf[sel].reshape(4, 128).T)
            cbr.append(cbf[sel][:384])
            dtu = np.concatenate([A["dt"][b][:, 4 * g:4 * g + 4], A["dt"][b][:, 32 + 4 * g:32 + 4 * g + 4]], axis=1)
            dts.append(dtu.reshape(SEQ // 128, 128, 8).transpose(1, 0, 2).reshape(128, (SEQ // 128) * 8))
            als.append(np.concatenate([W["ssd_a_log_f"][0][4 * g:4 * g + 4], W["ssd_a_log_b"][0][4 * g:4 * g + 4]]))
            dks.append(W["ssd_d"][0][4 * g:4 * g + 4])
        ins.append({"xbcT": _c(np.concatenate(xs, axis=0)), "cw": _c(np.concatenate(cws, axis=0)), "cb": _c(np.concatenate(cbs_, axis=0)),
                    "dt": _c(np.concatenate(dts, axis=0)), "alog": _c(np.stack(als)), "dsk": _c(np.stack(dks))})
    r = _run(build_b2(), ins)
    yf = np.zeros((2, SEQ, 2048), np.float32)
    yb = np.zeros((2, SEQ, 2048), np.float32)
    for c in range(8):
        b = c // 4
        for u in range(2):
            g = 2 * (c % 4) + u
            yf[b][:, g * 256:(g + 1) * 256] = r[c]["yf"][u * SEQ:(u + 1) * SEQ]
            yb[b][:, g * 256:(g + 1) * 256] = r[c]["yb"][u * SEQ:(u + 1) * SEQ]
    return yf, yb


def stage_ac23(W, A, yf, yb, rope):
    gq = _c(np.concatenate([np.tile(W["dil_g_q"][0][g], 16) for g in range(3)])[None, :])
    gk = _c(np.concatenate([np.tile(W["dil_g_k"][0][g], 16) for g in range(3)])[None, :])
    ins = []
    for c in range(8):
        b, sl = tok_slice(c)
        d = {"h": _c(A["h"][b, sl]), "yf": _c(yf[b, sl]), "yb": _c(yb[b, sl]), "sz": _c(A["sz"][b, sl]),
             "g_rep": W["ssd_g_norm"][0:1], "w_out": W["ssd_w_out"][0], "norm_mix": W["norm_mix"][3:4],
             "w_in": W["dil_w_in"][0], "gq_rep": gq, "gk_rep": gk, "rope": _c(rope[sl])}
        d.update(common_c_inputs(W, 2, c))
        ins.append(d)
    r = _run(build_ac(2, 3), ins)
    return {n: _gather(r, m) for n, m in (("h", "h_out"), ("q", "q"), ("k", "k"), ("v", "v"))}


def stage_b3(A):
    ins = []
    meta = []
    for c in range(8):
        b = c // 4
        qs, ks, vs = [], [], []
        um = []
        for g in range(3):
            dil = DIL_LIST[g]
            L = SEQ // dil
            for j in range(4):
                hd = 4 * (c % 4) + j
                cs = slice(g * 1024 + hd * 64, g * 1024 + (hd + 1) * 64)
                q = A["q"][b][:, cs]; k = A["k"][b][:, cs]; v = A["v"][b][:, cs]
                qs.append(np.concatenate([q[r::dil] for r in range(dil)], axis=0).T)
                kp = np.concatenate([np.pad(k[r::dil], ((64, 64), (0, 0))) for r in range(dil)], axis=0)
                vpd = np.concatenate([np.pad(v[r::dil], ((64, 64), (0, 0))) for r in range(dil)], axis=0)
                ks.append(np.pad(kp, ((0, KPAD - kp.shape[0]), (0, 0))).T)
                vs.append(np.pad(vpd, ((0, KPAD - vpd.shape[0]), (0, 0))))
                um.append((g, hd, dil, L))
        meta.append(um)
        ins.append({"qT": _c(np.concatenate(qs, axis=0)), "kT": _c(np.concatenate(ks, axis=0)), "vp": _c(np.concatenate(vs, axis=0))})
    r = _run(build_b3(), ins)
    oa = np.zeros((2, SEQ, 3, 16, 65), np.float32)
    for c in range(8):
        b = c // 4
        o = np.asarray(r[c]["o"]).reshape(12, SEQ, 65)
        for u, (g, hd, dil, L) in enumerate(meta[c]):
            for rr in range(dil):
                oa[b, rr::dil, g, hd, :] = o[u, rr * L:(rr + 1) * L]
    return oa.reshape(2, SEQ, 3 * 16 * 65)


def stage_c3(W, A, oa):
    ins = []
    for c in range(8):
        b, sl = tok_slice(c)
        d = {"h": _c(A["h"][b, sl]), "oa": _c(oa[b, sl]), "w_out": W["dil_w_out"][0]}
        d.update(common_c_inputs(W, 3, c))
        ins.append(d)
    r = _run(build_ac(3, None), ins)
    return _gather(r, "out")


def kernel(**inp):
    W = {k: np.asarray(v) for k, v in inp.items()}
    x = W["x"]
    rope = rope_table()
    A0 = stage_a0(W, x)
    of, ob = stage_b0(A0)
    A1 = stage_ac01(W, A0, of, ob, rope)
    _DEBUG["h0"] = A1["h"]
    on = stage_b1(W, A1)
    A2 = stage_ac12(W, A1, on)
    _DEBUG["h1"] = A2["h"]
    yf, yb = stage_b2(W, A2)
    A3 = stage_ac23(W, A2, yf, yb, rope)
    _DEBUG["h2"] = A3["h"]
    oa = stage_b3(A3)
    out = stage_c3(W, A3, oa)
    out = np.ascontiguousarray(out.astype(np.float32))
    if os.environ.get("KDBG_DIR"):
        for kk, vv in _DEBUG.items():
            np.save(os.path.join(os.environ["KDBG_DIR"], "kdbg_%s.npy" % kk), vv)
    return out
```
